# Optimizing a Trainium2 kernel written in Bass

```python
import jax, jax.numpy as jnp
from jax import lax
import numpy as np

D_MODEL = 1024
BATCH = 4
SEQ = 4096
DEPTH = 2
DEC_BATCH = 8
DEC_SEQ = 2048
PAST_LEN = 128

N_MIXERS = 2
CONV_WIDTH = 3
RET_HEADS = D_MODEL // 256
RET_QK_DIM = D_MODEL // RET_HEADS
RET_V_DIM = 2 * RET_QK_DIM
RET_QK_WIDTH = RET_HEADS * RET_QK_DIM
RET_V_WIDTH = RET_HEADS * RET_V_DIM
CHUNK = 128
D_FF = 4 * D_MODEL
NORM_EPS = 1e-6
ROPE_BASE = 10000.0

kernel_name = 'bidir_shortconv_retention_hybrid_encoder'


def rms_norm(x, g):
    xf = x.astype(jnp.float32)
    y = xf * lax.rsqrt(jnp.mean(xf * xf, axis=-1, keepdims=True) + NORM_EPS)
    return (y * g.astype(jnp.float32)).astype(x.dtype)


def short_conv_mixer(x, w_in, conv_w, conv_b, w_out):
    s = x.shape[1]
    h, gate_b, gate_c = jnp.split(x @ w_in, 3, axis=-1)
    u = gate_c * h
    pad = CONV_WIDTH // 2
    up = jnp.pad(u, ((0, 0), (pad, pad), (0, 0)))
    z = conv_b + sum(up[:, j:j + s] * conv_w[j] for j in range(CONV_WIDTH))
    return (gate_b * z) @ w_out


def rotary(x, pos):
    half = x.shape[-1] // 2
    inv = ROPE_BASE ** (-jnp.arange(half, dtype=jnp.float32) / half)
    ang = pos[:, None] * inv[None, :]
    cos, sin = jnp.cos(ang), jnp.sin(ang)
    x1, x2 = x[..., :half], x[..., half:]
    return jnp.concatenate([x1 * cos - x2 * sin, x1 * sin + x2 * cos], axis=-1)


def retention_log_decays():
    h = jnp.arange(RET_HEADS, dtype=jnp.float32)
    fwd = jnp.log(1.0 - jnp.power(2.0, -5.0 - h))
    bwd = jnp.log(1.0 - jnp.power(2.0, -5.5 - h))
    return fwd, bwd


def retention_scan(q, k, v, log_gamma, strict):
    b, h, s, dk = q.shape
    dv = v.shape[-1]
    n = s // CHUNK

    def to_chunks(t):
        return t.reshape(b, h, n, CHUNK, t.shape[-1]).transpose(2, 0, 1, 3, 4)

    qc, kc, vc = to_chunks(q), to_chunks(k), to_chunks(v)
    idx = jnp.arange(CHUNK, dtype=jnp.float32)
    diff = idx[:, None] - idx[None, :]
    keep = diff > 0 if strict else diff >= 0
    intra_decay = jnp.where(keep, jnp.exp(log_gamma[:, None, None] * jnp.maximum(diff, 0.0)), 0.0)
    q_decay = jnp.exp(log_gamma[:, None] * (idx + 1.0))[..., None]
    k_decay = jnp.exp(log_gamma[:, None] * (CHUNK - 1.0 - idx))[..., None]
    chunk_decay = jnp.exp(log_gamma * CHUNK)[:, None, None]

    def step(state, qkv):
        qi, ki, vi = qkv
        scores = jnp.einsum('bhid,bhjd->bhij', qi, ki) * intra_decay
        out = jnp.einsum('bhij,bhjv->bhiv', scores, vi)
        out = out + jnp.einsum('bhid,bhdv->bhiv', qi * q_decay, state)
        state = state * chunk_decay + jnp.einsum('bhjd,bhjv->bhdv', ki * k_decay, vi)
        return state, out

    state0 = jnp.zeros((b, h, dk, dv), jnp.float32)
    _, out = lax.scan(step, state0, (qc, kc, vc))
    return out.transpose(1, 2, 0, 3, 4).reshape(b, h, s, dv)


def retention_mixer(x, w_qkvg, w_o):
    b, s, _ = x.shape
    q, k, v, g = jnp.split(x @ w_qkvg, [RET_QK_WIDTH, 2 * RET_QK_WIDTH, 2 * RET_QK_WIDTH + RET_V_WIDTH], axis=-1)

    def heads(t, d):
        return t.reshape(b, s, RET_HEADS, d).transpose(0, 2, 1, 3).astype(jnp.float32)

    pos = jnp.arange(s, dtype=jnp.float32)
    q = rotary(heads(q, RET_QK_DIM), pos) * (RET_QK_DIM ** -0.5)
    k = rotary(heads(k, RET_QK_DIM), pos)
    v = heads(v, RET_V_DIM)
    lg_fwd, lg_bwd = retention_log_decays()
    rev = lambda t: jnp.flip(t, axis=2)
    o_fwd = retention_scan(q, k, v, lg_fwd, strict=False)
    o_bwd = rev(retention_scan(rev(q), rev(k), rev(v), lg_bwd, strict=True))
    o = o_fwd + o_bwd
    o = o - jnp.mean(o, axis=-1, keepdims=True)
    o = o * lax.rsqrt(jnp.mean(o * o, axis=-1, keepdims=True) + NORM_EPS)
    o = o.transpose(0, 2, 1, 3).reshape(b, s, RET_V_WIDTH).astype(x.dtype)
    return (jax.nn.silu(g) * o) @ w_o


def sq_relu_mlp(x, w_up, w_down):
    return jnp.square(jax.nn.relu(x @ w_up)) @ w_down


def encoder_trunk(x, norm_mix_0, w_in_conv_0, conv_w_0, conv_b_0, w_out_conv_0, norm_mlp_0, w_up_0, w_down_0,
                  norm_mix_1, w_qkvg_1, w_o_1, norm_mlp_1, w_up_1, w_down_1, norm_final):
    mixer_fns = [
        lambda t: short_conv_mixer(t, w_in_conv_0, conv_w_0, conv_b_0, w_out_conv_0),
        lambda t: retention_mixer(t, w_qkvg_1, w_o_1),
    ]
    mix_norms = [norm_mix_0, norm_mix_1]
    mlp_norms = [norm_mlp_0, norm_mlp_1]
    ups = [w_up_0, w_up_1]
    downs = [w_down_0, w_down_1]
    for i in range(DEPTH):
        x = x + mixer_fns[i % N_MIXERS](rms_norm(x, mix_norms[i]))
        x = x + sq_relu_mlp(rms_norm(x, mlp_norms[i]), ups[i], downs[i])
    return rms_norm(x, norm_final)


def setup_inputs(seed: int = 0) -> dict:
    key = jax.random.key(seed)
    ks = jax.random.split(key, 20)
    f32 = jnp.float32

    def w(k, shape, fan_in):
        return jax.random.normal(k, shape, f32) * (fan_in ** -0.5)

    def gain(k):
        return jnp.ones((D_MODEL,), f32) + 0.02 * jax.random.normal(k, (D_MODEL,), f32)

    return {
        'x_prompt': jax.random.normal(ks[0], (BATCH, SEQ, D_MODEL), f32),
        'x_sample': jax.random.normal(ks[1], (DEC_BATCH, DEC_SEQ, D_MODEL), f32),
        'norm_mix_0': gain(ks[2]),
        'w_in_conv_0': w(ks[3], (D_MODEL, 3 * D_MODEL), D_MODEL),
        'conv_w_0': w(ks[4], (CONV_WIDTH, D_MODEL), CONV_WIDTH),
        'conv_b_0': 0.02 * jax.random.normal(ks[5], (D_MODEL,), f32),
        'w_out_conv_0': w(ks[6], (D_MODEL, D_MODEL), D_MODEL),
        'norm_mlp_0': gain(ks[7]),
        'w_up_0': w(ks[8], (D_MODEL, D_FF), D_MODEL),
        'w_down_0': w(ks[9], (D_FF, D_MODEL), D_FF),
        'norm_mix_1': gain(ks[10]),
        'w_qkvg_1': w(ks[11], (D_MODEL, 2 * RET_QK_WIDTH + 2 * RET_V_WIDTH), D_MODEL),
        'w_o_1': w(ks[12], (RET_V_WIDTH, D_MODEL), RET_V_WIDTH),
        'norm_mlp_1': gain(ks[13]),
        'w_up_1': w(ks[14], (D_MODEL, D_FF), D_MODEL),
        'w_down_1': w(ks[15], (D_FF, D_MODEL), D_FF),
        'norm_final': gain(ks[16]),
    }


def reference(x_prompt, x_sample, norm_mix_0, w_in_conv_0, conv_w_0, conv_b_0, w_out_conv_0, norm_mlp_0,
              w_up_0, w_down_0, norm_mix_1, w_qkvg_1, w_o_1, norm_mlp_1, w_up_1, w_down_1, norm_final):
    y_prompt = encoder_trunk(x_prompt, norm_mix_0, w_in_conv_0, conv_w_0, conv_b_0, w_out_conv_0, norm_mlp_0,
                             w_up_0, w_down_0, norm_mix_1, w_qkvg_1, w_o_1, norm_mlp_1, w_up_1, w_down_1,
                             norm_final)
    y_sample = encoder_trunk(x_sample, norm_mix_0, w_in_conv_0, conv_w_0, conv_b_0, w_out_conv_0, norm_mlp_0,
                             w_up_0, w_down_0, norm_mix_1, w_qkvg_1, w_o_1, norm_mlp_1, w_up_1, w_down_1,
                             norm_final)
    return (y_prompt, y_sample)
```

```python
import contextlib
import math
import numpy as np
import concourse.bass as bass
import concourse.mybir as mybir
from concourse.bass_utils import run_bass_kernel_spmd

F32 = mybir.dt.float32
BF16 = mybir.dt.bfloat16
AF = mybir.ActivationFunctionType
ALU = mybir.AluOpType

D = 1024
NTOK = 4096
T = 512
NT = NTOK // T
NCH = NTOK // 128
EPS = 1e-6
NSLOT = 3
LGF = [math.log(1.0 - 2.0 ** (-5.0 - h)) for h in range(4)]
LGB = [math.log(1.0 - 2.0 ** (-5.5 - h)) for h in range(4)]
QS = 256 ** -0.5
PI = math.pi
PI_LO = 3.1415925

DEBUG = False
STAGE = ["init"]
PE_LABELS = []


class Region:
    __slots__ = ("name", "w", "r", "psum")

    def __init__(self, name, psum=False):
        self.name = name
        self.w = None
        self.r = {}
        self.psum = psum


class Chan:
    def __init__(self, sem):
        self.sem = sem
        self.count = 0


class Eng:
    def __init__(self, h, sem, is_pe=False):
        self.h = h
        self.sem = sem
        self.cnt = 0
        self.waited = {}
        self.is_pe = is_pe

    def _wait(self, tok):
        sem, val = tok
        if sem is self.sem and (self.is_pe or self.sem is None):
            return
        key = id(sem)
        if self.waited.get(key, 0) >= val:
            return
        self.h.wait_ge(sem, val)
        self.waited[key] = val

    def _deps(self, reads, writes):
        toks = {}
        for R in reads:
            if R.w is not None:
                toks[(id(R.w[0]), R.w[1])] = R.w
            if R.psum:
                for t in R.r.values():
                    if t[0] is not self.sem:
                        toks[(id(t[0]), t[1])] = t
        for R in writes:
            if R.w is not None:
                toks[(id(R.w[0]), R.w[1])] = R.w
            for t in R.r.values():
                toks[(id(t[0]), t[1])] = t
        for t in toks.values():
            self._wait(t)

    @staticmethod
    def _mark(reads, writes, tok):
        k = id(tok[0])
        for R in reads:
            old = R.r.get(k)
            if old is None or old[1] < tok[1]:
                R.r[k] = tok
        for R in writes:
            R.w = tok
            R.r = {}

    def op(self, fn, reads=(), writes=(), signal=True):
        self._deps(reads, writes)
        inst = fn()
        if self.is_pe:
            PE_LABELS.append(STAGE[0])
        if signal:
            self.cnt += 1
            inst.then_inc(self.sem, 1)
            tok = (self.sem, self.cnt)
        else:
            tok = (self.sem, self.cnt + 1)
        self._mark(reads, writes, tok)
        return tok

    def dma(self, chan, out, in_, reads=(), writes=(), n=1, **kw):
        self._deps(reads, writes)
        if n == 1:
            outs, ins = [out], [in_]
        else:
            outs, ins = out, in_
        for o, i in zip(outs, ins):
            self.h.dma_start(out=o, in_=i, **kw).then_inc(chan.sem, 16)
            chan.count += 16
        tok = (chan.sem, chan.count)
        self._mark(reads, writes, tok)
        return tok


def build_nc():
    nc = bass.Bass("TRN2", target_bir_lowering=False)
    es = contextlib.ExitStack()

    def dram(name, shape, dt, kind):
        return nc.dram_tensor(name, shape, dt, kind=kind)

    x_in = dram("x", [NTOK, D], F32, "ExternalInput").ap()
    w_in_d = dram("w_in_conv_0", [D, 3 * D], F32, "ExternalInput").ap()
    w_outc_d = dram("w_out_conv_0", [D, D], F32, "ExternalInput").ap()
    w_up0_d = dram("w_up_0", [D, 4 * D], F32, "ExternalInput").ap()
    w_dn0_d = dram("w_down_0", [4 * D, D], F32, "ExternalInput").ap()
    w_qkvg_d = dram("w_qkvg_1", [D, 6 * D], F32, "ExternalInput").ap()
    w_o_d = dram("w_o_1", [2 * D, D], F32, "ExternalInput").ap()
    w_up1_d = dram("w_up_1", [D, 4 * D], F32, "ExternalInput").ap()
    w_dn1_d = dram("w_down_1", [4 * D, D], F32, "ExternalInput").ap()
    small_d = dram("small", [128, 72], F32, "ExternalInput").ap()
    flag_d = dram("flag", [128, 2], F32, "ExternalInput").ap()
    inv_d = dram("rope_inv", [128, 1], F32, "ExternalInput").ap()
    y_out = dram("y", [NTOK, D], F32, "ExternalOutput").ap()

    skind = "ExternalOutput" if DEBUG else "Internal"
    NSLAB = 58
    wscr = dram("wscr", [NSLAB, 128, 4096], BF16, "Internal").ap()
    x1_scr = dram("x1_scr", [NT, 128, 8, T], F32, skind).ap()
    cs_scr = dram("cs_scr", [NT, 128, 2, T], F32, skind).ap()
    k_scr = dram("k_scr", [NT, 128, 8, T], BF16, skind).ap()
    kf_scr = dram("kf_scr", [NCH, 128, 1024], BF16, skind).ap()
    v_scr = dram("v_scr", [NCH, 128, 2048], BF16, skind).ap()
    sb_scr = dram("sb_scr", [NCH, 128, 8, T], BF16, skind).ap()

    with es:
        def sb(name, shape, dt):
            return es.enter_context(nc.sbuf_tensor("sb_" + name, shape, dt))

        def new_sem(name):
            return es.enter_context(nc.semaphore(name))

        def chan(name):
            return Chan(new_sem(name))

        PE = Eng(nc.tensor, new_sem("s_pe"), is_pe=True)
        ACT = Eng(nc.scalar, new_sem("s_act"))
        DVE = Eng(nc.vector, new_sem("s_dve"))
        POOL = Eng(nc.gpsimd, new_sem("s_pool"))
        SP = Eng(nc.sync, None)

        wring = sb("wring", [128, NSLOT, 4096], BF16)
        slot_R = [Region(f"slot{i}") for i in range(NSLOT)]
        slot_ch = [chan(f"c_slot{i}") for i in range(NSLOT)]

        ident = sb("ident", [128, 128], F32)
        identb = sb("identb", [128, 128], BF16)
        onesb = sb("onesb", [128, 128], BF16)
        DT = sb("DT", [128, 4, 128], F32)
        decqf = sb("decqf", [128, 4, 128], F32)
        decqb = sb("decqb", [128, 4, 128], F32)
        kdf = sb("kdf", [128, 1024], F32)
        kdb = sb("kdb", [128, 1024], F32)
        small = sb("small", [128, 72], F32)
        flagt = sb("flagt", [128, 2], F32)
        invt = sb("invt", [128, 1], F32)
        R_const = Region("const")
        R_kd = Region("kd")

        xT = sb("xT", [128, 8, T], F32)
        xT_R = [Region(f"xT{k}") for k in range(8)]
        xn = sb("xn", [128, 8, T], BF16)
        xn_R = [Region(f"xn{k}") for k in range(8)]
        rstd = sb("rstd", [128, 3, T], F32)
        rstd_R = [Region("rstd0"), Region("rstd1"), Region("rstd2")]
        cst = sb("cst", [128, 2, T], F32)
        cst_R = Region("cst")
        c_cst = chan("c_cst")
        S = sb("S", [128, 8, T], F32)
        S_R = [Region(f"S{i}") for i in range(8)]
        Sbf = sb("Sbf", [128, 2, 8, T], BF16)
        Sbf_R = [[Region(f"Sbf{b}_{i}") for i in range(8)] for b in range(2)]
        c_sbf = [chan("c_sbf0"), chan("c_sbf1")]

        hm = sb("hm", [128, 32, T], BF16)
        HR = [Region(f"hm{j}") for j in range(32)]

        def hm_f32(j):
            return hm[:, 2 * j:2 * j + 2, :].rearrange("p a b -> p (a b)").bitcast(F32)

        misc = sb("misc", [128, 64], F32)
        misc_R = [Region(f"misc{i}") for i in range(16)]
        R_tbl = Region("tblswitch")

        def act_preload(func):
            ACT.op(lambda: nc.scalar.activation(misc[:, 62:63], misc[:, 60:61], func, bias=1.0),
                   reads=[R_tbl], writes=[R_tbl])

        banks = [es.enter_context(nc.psum_tensor(f"bank{i}", [128, T], F32)) for i in range(8)]
        bank_R = [Region(f"bank{i}", psum=True) for i in range(8)]
        BG = {"main": [0, 1, 2, 3, 4, 5], "side": [6, 7], "o": [0, 1, 2, 3], "r": [4, 5, 6, 7]}
        bstate = {g: 0 for g in BG}

        def bank(g="main"):
            lst = BG[g]
            i = lst[bstate[g] % len(lst)]
            bstate[g] += 1
            return banks[i], bank_R[i]

        slabs = []

        def add_cols(name, W, K, c0, idx):
            KC = K // 128
            C = 4096 // KC
            src = W[:, c0:c0 + C].rearrange("(kc p) c -> p kc c", p=128)
            slabs.append(dict(name=f"{name}{idx}", KC=KC, C=C, used=4096, src=src, kind="cols"))

        SL = {}
        for c in range(8):
            src = w_in_d.rearrange("(kc p) (g f) -> p kc g f", p=128, g=3)[:, :, :, c * 128:(c + 1) * 128]
            SL[("win", c)] = len(slabs)
            slabs.append(dict(name=f"win{c}", KC=8, C=384, used=3072, src=src, kind="win"))
        for i in range(2):
            SL[("wout", i)] = len(slabs); add_cols("wout", w_outc_d, D, i * 512, i)
        for i in range(8):
            SL[("up0", i)] = len(slabs); add_cols("up0", w_up0_d, D, i * 512, i)
        for i in range(8):
            SL[("dn0", i)] = len(slabs); add_cols("dn0", w_dn0_d, 4 * D, i * 128, i)
        for i in range(2):
            SL[("wk", i)] = len(slabs); add_cols("wk", w_qkvg_d, D, 1024 + i * 512, i)
        for i in range(4):
            SL[("wv", i)] = len(slabs); add_cols("wv", w_qkvg_d, D, 2048 + i * 512, i)
        for i in range(2):
            SL[("wq", i)] = len(slabs); add_cols("wq", w_qkvg_d, D, i * 512, i)
        for i in range(4):
            SL[("wg", i)] = len(slabs); add_cols("wg", w_qkvg_d, D, 4096 + i * 512, i)
        for i in range(4):
            SL[("wo", i)] = len(slabs); add_cols("wo", w_o_d, 2 * D, i * 256, i)
        for i in range(8):
            SL[("up1", i)] = len(slabs); add_cols("up1", w_up1_d, D, i * 512, i)
        for i in range(8):
            SL[("dn1", i)] = len(slabs); add_cols("dn1", w_dn1_d, 4 * D, i * 128, i)
        assert len(slabs) == NSLAB
        slab_R = [Region(f"wscr{i}") for i in range(NSLAB)]

        p1_list = ([("win", c) for c in range(8)] + [("wout", i) for i in range(2)] +
                   [("up0", i) for i in range(8)] + [("dn0", i) for i in range(8)] +
                   [("wk", i) for i in range(2)] + [("wv", i) for i in range(4)])
        p2_list = ([("wq", i) for i in range(2)] + [("wg", i) for i in range(4)] +
                   [("wo", i) for i in range(4)] + [("up1", i) for i in range(8)] +
                   [("dn1", i) for i in range(8)])
        seq = []
        for _ in range(NT):
            seq += p1_list
        for _ in range(NT):
            seq += p2_list
        order = [SL[k] for k in p1_list] + [SL[k] for k in p2_list]
        order_pos = {sid: i for i, sid in enumerate(order)}
        cst8 = {"n": 0}
        CAST_AHEAD = 5

        def ensure_casts(n):
            while cst8["n"] < min(n, len(order)):
                sid = order[cst8["n"]]
                cst8["n"] += 1
                sl = slabs[sid]
                ch = chan(f"c_cast{sid}")
                if sl["kind"] == "win":
                    dst = wscr[sid][:, 0:3072].rearrange("p (kc g f) -> p kc g f", kc=8, g=3)
                    POOL.dma(ch, [dst[:, :, g, :] for g in range(3)], [sl["src"][:, :, g, :] for g in range(3)],
                             writes=[slab_R[sid]], n=3)
                else:
                    KC = sl["KC"]
                    dst = wscr[sid].rearrange("p (kc c) -> p kc c", kc=KC)
                    if KC == 32:
                        POOL.dma(ch, [dst[:, 0:16, :], dst[:, 16:32, :]],
                                 [sl["src"][:, 0:16, :], sl["src"][:, 16:32, :]],
                                 writes=[slab_R[sid]], n=2)
                    else:
                        POOL.dma(ch, dst, sl["src"], writes=[slab_R[sid]])

        wst = {"issued": 0, "next": 0}

        def w_issue_upto(n):
            while wst["issued"] < min(n, len(seq)):
                i = wst["issued"]
                sid = SL[seq[i]]
                ensure_casts(order_pos[sid] + 1 + CAST_AHEAD)
                s = i % NSLOT
                used = slabs[sid]["used"]
                SP.dma(slot_ch[s], wring[:, s, 0:used], wscr[sid][:, 0:used],
                       reads=[slab_R[sid]], writes=[slot_R[s]])
                wst["issued"] += 1

        def w_next(key):
            i = wst["next"]
            assert seq[i] == key, (seq[i], key)
            w_issue_upto(i + NSLOT)
            wst["next"] += 1
            s = i % NSLOT
            return wring[:, s, :], slot_R[s]

        c_cons = chan("c_cons")
        SP.dma(c_cons, [small[:], flagt[:], invt[:]], [small_d, flag_d, inv_d], writes=[R_const], n=3)

        def g_mix0(k): return small[:, 0 + k:1 + k]
        def g_mlp0(k): return small[:, 8 + k:9 + k]
        def g_mix1(k): return small[:, 16 + k:17 + k]
        def g_mlp1(k): return small[:, 24 + k:25 + k]
        def g_fin(k): return small[:, 32 + k:33 + k]
        def cw(j, k): return small[:, 40 + j * 8 + k:41 + j * 8 + k]
        def cb(k): return small[:, 64 + k:65 + k]
        flag_ap = flagt[:, 0:1]
        off_ap = flagt[:, 1:2]

        R_id = Region("ident")
        POOL.op(lambda: nc.gpsimd.iota(ident[:], [[1, 128]], base=0, channel_multiplier=-1,
                                       allow_small_or_imprecise_dtypes=True), writes=[R_id])
        POOL.op(lambda: nc.gpsimd.tensor_single_scalar(ident[:], ident[:], 0.0, ALU.is_equal),
                reads=[R_id], writes=[R_id])
        POOL.op(lambda: nc.gpsimd.tensor_copy(identb[:], ident[:]), reads=[R_id], writes=[R_const])
        POOL.op(lambda: nc.gpsimd.memset(onesb[:], 1.0), writes=[R_const])
        POOL.op(lambda: nc.gpsimd.memset(misc[:, 56:64], 0.0), writes=[R_tbl])
        POOL.op(lambda: nc.gpsimd.memset(S[:], 0.0), writes=S_R)
        POOL.op(lambda: nc.gpsimd.memset(Sbf[:], 0.0), writes=Sbf_R[0] + Sbf_R[1])

        tA = hm_f32(0); tB = hm_f32(1); tC = hm_f32(2); tD = hm_f32(3)
        RA = [HR[0], HR[1]]; RB = [HR[2], HR[3]]; RC = [HR[4], HR[5]]; RD = [HR[6], HR[7]]

        def gen_consts():
            dji = tA[:, 0:128]
            POOL.op(lambda: nc.gpsimd.iota(dji, [[1, 128]], base=0, channel_multiplier=-1,
                                           allow_small_or_imprecise_dtypes=True), writes=RA)
            idx = tA[:, 128:256]
            pidx = tA[:, 256:512]
            POOL.op(lambda: nc.gpsimd.iota(idx, [[1, 128]], base=0, channel_multiplier=0,
                                           allow_small_or_imprecise_dtypes=True), writes=RA)
            POOL.op(lambda: nc.gpsimd.iota(pidx, [[0, 256]], base=0, channel_multiplier=1,
                                           allow_small_or_imprecise_dtypes=True), writes=RA)
            a_ = tB[:, 0:128]; b_ = tB[:, 128:256]; mf = tB[:, 256:384]; mb = tB[:, 384:512]
            DVE.op(lambda: nc.vector.tensor_scalar_max(a_, dji, 0.0), reads=RA, writes=RB)
            DVE.op(lambda: nc.vector.tensor_scalar(b_, dji, -1.0, 0.0, ALU.mult, ALU.max), reads=RA, writes=RB)
            DVE.op(lambda: nc.vector.tensor_single_scalar(mf, dji, 0.0, ALU.is_ge), reads=RA, writes=RB)
            DVE.op(lambda: nc.vector.tensor_single_scalar(mb, dji, 0.0, ALU.is_lt), reads=RA, writes=RB)
            for h in range(4):
                ef = tC[:, 0:128]; eb = tC[:, 128:256]
                ACT.op(lambda: nc.scalar.activation(ef, a_, AF.Exp, scale=LGF[h]), reads=RB, writes=RC)
                ACT.op(lambda: nc.scalar.activation(eb, b_, AF.Exp, scale=LGB[h]), reads=RB, writes=RC)
                DVE.op(lambda: nc.vector.tensor_tensor(ef, ef, mf, ALU.mult), reads=RB + RC, writes=RC)
                DVE.op(lambda: nc.vector.tensor_tensor(eb, eb, mb, ALU.mult), reads=RB + RC, writes=RC)
                DVE.op(lambda: nc.vector.scalar_tensor_tensor(DT[:, h, :], ef, 1.0, eb, ALU.mult, ALU.add),
                       reads=RC, writes=[R_const])
                DVE.op(lambda: nc.vector.tensor_scalar_mul(DT[:, h, :], DT[:, h, :], QS),
                       reads=[R_const], writes=[R_const])
                ACT.op(lambda: nc.scalar.activation(decqf[:, h, :], idx, AF.Exp, scale=LGF[h],
                                                    bias=LGF[h] + math.log(QS)), reads=RA, writes=[R_const])
                ACT.op(lambda: nc.scalar.activation(decqb[:, h, :], idx, AF.Exp, scale=-LGB[h],
                                                    bias=128.0 * LGB[h] + math.log(QS)), reads=RA, writes=[R_const])
                ACT.op(lambda: nc.scalar.activation(kdf[:, h * 256:(h + 1) * 256], pidx, AF.Exp,
                                                    scale=-LGF[h], bias=127.0 * LGF[h]), reads=RA, writes=[R_kd])
                ACT.op(lambda: nc.scalar.activation(kdb[:, h * 256:(h + 1) * 256], pidx, AF.Exp,
                                                    scale=LGB[h]), reads=RA, writes=[R_kd])

        kint = sb("kint", [128, T], mybir.dt.int32)
        R_ki = Region("kint")
        cs_R = [Region(f"cs_scr{t}") for t in range(NT)]
        c_csst = chan("c_csst")
        def gen_cs_iota(t):
            POOL.op(lambda: nc.gpsimd.iota(tD, [[1, T]], base=t * T, channel_multiplier=0,
                                           allow_small_or_imprecise_dtypes=True), writes=RD)

        def gen_cs(t, do_iota=True):
            pos = tD
            if do_iota:
                gen_cs_iota(t)
            if t >= NT // 2:
                DVE.op(lambda: nc.vector.tensor_scalar(pos, pos, off_ap, None, ALU.subtract),
                       reads=RD + [R_const], writes=RD)
            DVE.op(lambda: nc.vector.tensor_scalar(pos, pos, invt[:, 0:1], None, ALU.mult),
                   reads=RD + [R_const], writes=RD)
            for half, shift in ((0, 0.25), (1, 0.0)):
                tt = tB; ki = kint[:]; kf = tA
                DVE.op(lambda: nc.vector.tensor_scalar(tt, pos, 1.0 / (2.0 * PI), shift, ALU.mult, ALU.add),
                       reads=RD, writes=RB)
                DVE.op(lambda: nc.vector.tensor_copy(ki, tt), reads=RB, writes=[R_ki])
                DVE.op(lambda: nc.vector.tensor_copy(kf, ki), reads=[R_ki], writes=RA)
                DVE.op(lambda: nc.vector.tensor_tensor(tt, tt, kf, ALU.subtract), reads=RA + RB, writes=RB)
                DVE.op(lambda: nc.vector.tensor_single_scalar(kf, tt, 0.5, ALU.is_gt), reads=RB, writes=RA)
                DVE.op(lambda: nc.vector.tensor_tensor(tt, tt, kf, ALU.subtract), reads=RA + RB, writes=RB)
                DVE.op(lambda: nc.vector.tensor_scalar(cst[:, half, :], tt, 2.0 * PI_LO, None, ALU.mult),
                       reads=RB, writes=[cst_R])
            ACT.op(lambda: nc.scalar.activation(cst[:], cst[:], AF.Sin), reads=[cst_R], writes=[cst_R])
            act_preload(AF.Ln)
            SP.dma(c_csst, cs_scr[t], cst[:], reads=[cst_R], writes=[cs_R[t]])

        def rms_sq(xs, xs_R, sq, sq_R):
            for k in range(8):
                ACT.op(lambda: nc.scalar.activation(sq[k], xs[k], AF.Square),
                       reads=[xs_R[k]], writes=[sq_R[k]])

        def rmsnorm(N, xs, xs_R, gfn, outs, outs_R, rs, rs_R, sq=None, sq_R=None, do_sq=True):
            if sq is None:
                sq, sq_R = outs, outs_R
            bk, bR = bank("side")
            if do_sq:
                rms_sq(xs, xs_R, sq, sq_R)
            for k in range(8):
                PE.op(lambda: nc.tensor.matmul(bk[:, 0:N], onesb[:], sq[k], start=(k == 0), stop=(k == 7)),
                      reads=[sq_R[k], R_const], writes=[bR], signal=(k == 7))
            ACT.op(lambda: nc.scalar.activation(rs, bk[:, 0:N], AF.Ln, bias=EPS, scale=1.0 / D),
                   reads=[bR], writes=[rs_R])
            ACT.op(lambda: nc.scalar.activation(rs, rs, AF.Exp, scale=-0.5), reads=[rs_R], writes=[rs_R])
            for k in range(8):
                DVE.op(lambda: nc.vector.scalar_tensor_tensor(outs[k], xs[k], gfn(k), rs, ALU.mult, ALU.mult),
                       reads=[xs_R[k], rs_R, R_const], writes=[outs_R[k]])

        WARM = 24

        def keep_warm(bk, bR, n=None):
            prev = STAGE[0]
            STAGE[0] = "warm"
            for _ in range(WARM if n is None else n):
                PE.op(lambda: nc.tensor.matmul(bk[:, 0:128], onesb[:], onesb[:], start=True, stop=True),
                      reads=[R_const], writes=[bR], signal=False)
            STAGE[0] = prev

        def proj_fm(slab, sR, KC, ncols_chunks, rhs, rhs_R, evac, warm=False):
            C = 4096 // KC
            for mm in range(ncols_chunks):
                bk, bR = bank()
                if warm and mm == 0:
                    keep_warm(bk, bR)
                for k in range(KC):
                    PE.op(lambda: nc.tensor.matmul(bk[:], slab[:, k * C + mm * 128:k * C + (mm + 1) * 128],
                                                   rhs[k], start=(k == 0), stop=(k == KC - 1)),
                          reads=[sR, rhs_R[k]], writes=[bR], signal=(k == KC - 1))
                evac(mm, bk, bR)

        xT_k = [xT[:, k, :] for k in range(8)]
        xn_k = [xn[:, k, :] for k in range(8)]
        hm_k = [hm[:, j, :] for j in range(32)]

        def mlp(up_name, dn_name, gfn, XK=None, XR=None, mid_hook=None, slab_hook=None):
            if XK is None:
                XK, XR = xT_k, xT_R
            base = STAGE[0]
            STAGE[0] = base + ".N2"
            rmsnorm(T, XK, XR, gfn, xn_k, xn_R, rstd[:, 0, :], rstd_R[0])
            STAGE[0] = base + ".up"
            for i in range(8):
                slab, sR = w_next((up_name, i))

                def evac(mm, bk, bR, i=i):
                    j = i * 4 + mm
                    rb = 1 + (j % 2)
                    rt = rstd[:, rb, :]
                    ACT.op(lambda: nc.scalar.activation(rt, bk[:], AF.Relu), reads=[bR], writes=[rstd_R[rb]])
                    POOL.op(lambda: nc.gpsimd.tensor_tensor(hm_k[j], rt, rt, ALU.mult),
                            reads=[rstd_R[rb]], writes=[HR[j]])
                proj_fm(slab, sR, 8, 4, xn_k, xn_R, evac, warm=(i == 0))
                if slab_hook is not None:
                    slab_hook(i)
            if mid_hook is not None:
                mid_hook()
            STAGE[0] = base + ".dn"
            for i in range(8):
                slab, sR = w_next((dn_name, i))

                def evac(mm, bk, bR, i=i):
                    DVE.op(lambda: nc.vector.tensor_tensor(XK[i], bk[:], XK[i], ALU.add),
                           reads=[bR, XR[i]], writes=[XR[i]])
                proj_fm(slab, sR, 32, 1, hm_k, HR, evac)

        rot = sb("rot", [128, 4, T], F32)
        rot_R = [Region(f"rot{i}") for i in range(4)]
        xe = rot[0:16, 0:2, :].rearrange("p a b -> p (a b)")
        xeT = sb("xeT", [128, 8, 16], F32)
        xne = sb("xne", [128, 8, 16], BF16)
        uedge = sb("uedge", [128, 8, 16], F32)
        R_xeT = [Region(f"xeT{k}") for k in range(8)]
        R_xne = [Region(f"xne{k}") for k in range(8)]; R_ue = Region("uedge")
        c_xe = chan("c_xe")
        R_xe = rot_R[0]
        POOL.op(lambda: nc.gpsimd.memset(xe, 0.0), writes=[rot_R[0], rot_R[1]])
        SP.dma(c_xe, [xe[2 * (b - 1):2 * b, :] for b in range(1, 8)],
               [x_in[T * b - 1:T * b + 1, :] for b in range(1, 8)], writes=[R_xe], n=7)
        STAGE[0] = "edge"
        bk, bR = bank()
        for k in range(8):
            PE.op(lambda: nc.tensor.transpose(bk[:, k * 16:(k + 1) * 16], xe[0:16, k * 128:(k + 1) * 128],
                                              ident[0:16, 0:16]),
                  reads=[R_xe, R_id], writes=[bR], signal=(k == 7))
        DVE.op(lambda: nc.vector.tensor_copy(xeT[:].rearrange("p k n -> p (k n)"), bk[:, 0:128]),
               reads=[bR], writes=R_xeT)
        rmsnorm(16, [xeT[:, k, :] for k in range(8)], R_xeT, g_mix0,
                [xne[:, k, :] for k in range(8)], R_xne, rstd[:, 0, 0:16], rstd_R[0])
        R_uec = [Region(f"uedge{c}") for c in range(8)]

        def edge_chunk(c, slab, sR):
            prev = STAGE[0]
            STAGE[0] = "edge"
            bh, bhR = bank()
            bg, bgR = bank()
            for j, (bk_, bR_) in ((0, (bh, bhR)), (2, (bg, bgR))):
                for k in range(8):
                    PE.op(lambda: nc.tensor.matmul(bk_[:, 0:16], slab[:, k * 384 + j * 128:k * 384 + (j + 1) * 128],
                                                   xne[:, k, :], start=(k == 0), stop=(k == 7)),
                          reads=[sR, R_xne[k]], writes=[bR_], signal=(k == 7))
            he = misc[:, 0:16]
            ACT.op(lambda: nc.scalar.copy(he, bh[:, 0:16]), reads=[bhR], writes=[misc_R[0]])
            DVE.op(lambda: nc.vector.tensor_tensor(uedge[:, c, :], bg[:, 0:16], he, ALU.mult),
                   reads=[bgR, misc_R[0]], writes=[R_uec[c]])
            DVE.op(lambda: nc.vector.tensor_scalar(uedge[:, c, 6:8], uedge[:, c, 6:8], flag_ap, None, ALU.mult),
                   reads=[R_uec[c], R_const], writes=[R_uec[c]])
            STAGE[0] = prev

        xin = sb("xin", [128, 4, D], F32)
        xin_R = [Region(f"xin{c}") for c in range(4)]
        c_xin = [chan(f"c_xin{c}") for c in range(4)]
        kvA = sb("kvA", [128, 24, T], BF16)
        ktm = sb("ktm", [128, 1024], BF16)
        ktm_R = Region("ktm")
        vtm = sb("vtm", [128, 4, 2048], BF16)
        vtm_R = [[Region(f"v{c}_{h}") for h in range(4)] for c in range(4)]
        kfm = [kvA[:, m, :] for m in range(8)]
        kfm_R = [Region(f"kfm{m}") for m in range(8)]
        kft = [kvA[:, 8 + 2 * c:10 + 2 * c, :].rearrange("p a b -> p (a b)") for c in range(4)]
        kft_R = [Region(f"kft{c}") for c in range(4)]
        kbt = [kvA[:, 16 + 2 * c:18 + 2 * c, :].rearrange("p a b -> p (a b)") for c in range(4)]
        kbt_R = [Region(f"kbt{c}") for c in range(4)]
        c_x1st = chan("c_x1st"); c_kst = chan("c_kst"); c_kfst = chan("c_kfst"); c_vst = chan("c_vst")
        x1_R = [Region(f"x1_scr{t}") for t in range(NT)]
        ks_R = [Region(f"k_scr{t}") for t in range(NT)]
        kfs_R = [Region(f"kf_scr{t}") for t in range(NT)]
        vs_R = [Region(f"v_scr{t}") for t in range(NT)]
        sbs_R = [Region(f"sb_scr{n}") for n in range(NCH)]

        u_k = [hm_f32(c) for c in range(8)]
        u_R = [[HR[2 * c], HR[2 * c + 1]] for c in range(8)]
        gz_k = [hm[:, 16 + c, :] for c in range(8)]
        gz_R = [HR[16 + c] for c in range(8)]
        htmp = [hm_f32(12), hm_f32(13)]
        htmp_R = [[HR[24], HR[25]], [HR[26], HR[27]]]
        ztmp = [hm_f32(14), hm_f32(15)]
        ztmp_R = [[HR[28], HR[29]], [HR[30], HR[31]]]

        def load_cs(t):
            SP.dma(c_cst, cst[:], cs_scr[t], reads=[cs_R[t]], writes=[cst_R])

        def rotary(b1, b1R, b2, b2R, emit1, emit2, comb="pool"):
            cos = cst[:, 0, :]; sin = cst[:, 1, :]
            DVE.op(lambda: nc.vector.tensor_tensor(rot[:, 0, :], b1[:], cos, ALU.mult),
                   reads=[b1R, cst_R], writes=[rot_R[0]])
            DVE.op(lambda: nc.vector.tensor_tensor(rot[:, 1, :], b2[:], sin, ALU.mult),
                   reads=[b2R, cst_R], writes=[rot_R[1]])
            ce, ch_ = (POOL, nc.gpsimd) if comb == "pool" else (DVE, nc.vector)
            ce.op(lambda: ch_.tensor_tensor(rot[:, 0, :], rot[:, 0, :], rot[:, 1, :], ALU.subtract),
                  reads=[rot_R[0], rot_R[1]], writes=[rot_R[0]])
            emit1(rot[:, 0, :], rot_R[0])
            DVE.op(lambda: nc.vector.tensor_tensor(rot[:, 2, :], b1[:], sin, ALU.mult),
                   reads=[b1R, cst_R], writes=[rot_R[2]])
            DVE.op(lambda: nc.vector.tensor_tensor(rot[:, 3, :], b2[:], cos, ALU.mult),
                   reads=[b2R, cst_R], writes=[rot_R[3]])
            ce.op(lambda: ch_.tensor_tensor(rot[:, 2, :], rot[:, 2, :], rot[:, 3, :], ALU.add),
                  reads=[rot_R[2], rot_R[3]], writes=[rot_R[2]])
            emit2(rot[:, 2, :], rot_R[2])

        def state_update(n, kt, ktR, vt_c, vt_cR, gC, sbuf_i, boundary, grp="main", heads=(0, 1, 2, 3)):
            for h in heads:
                for dc in range(2):
                    i = h * 2 + dc
                    bk, bR = bank(grp)
                    PE.op(lambda: nc.tensor.matmul(bk[:], kt[:, h * 256 + dc * 128:h * 256 + (dc + 1) * 128],
                                                   vt_c[:, h * 512:(h + 1) * 512], start=True, stop=True),
                          reads=[ktR] + (vt_cR[h] if isinstance(vt_cR[h], list) else [vt_cR[h]]), writes=[bR])
                    DVE.op(lambda: nc.vector.scalar_tensor_tensor(S[:, i, :], S[:, i, :], gC[h], bk[:],
                                                                  ALU.mult, ALU.add),
                           reads=[bR, S_R[i]], writes=[S_R[i]])
                    if boundary:
                        DVE.op(lambda: nc.vector.tensor_scalar(S[:, i, :], S[:, i, :], flag_ap, None, ALU.mult),
                               reads=[S_R[i], R_const], writes=[S_R[i]])
                    ACT.op(lambda: nc.scalar.copy(Sbf[:, sbuf_i, i, :], S[:, i, :]),
                           reads=[S_R[i]], writes=[Sbf_R[sbuf_i][i]])

        GCF = [math.exp(128.0 * LGF[h]) for h in range(4)]
        GCB = [math.exp(128.0 * LGB[h]) for h in range(4)]

        sbf = {"cur": 0}

        def kT_chunk(t, c):
            prev = STAGE[0]
            STAGE[0] = "P1.S7kT"
            bk, bR = bank("side")
            bkb = bk[:].bitcast(BF16)
            for m in range(8):
                PE.op(lambda: nc.tensor.transpose(bkb[:, m * 128:(m + 1) * 128],
                                                  kfm[m][:, c * 128:(c + 1) * 128], identb[:]),
                      reads=[kfm_R[m], R_const], writes=[bR], signal=(m == 7))
            for h in range(4):
                ACT.op(lambda: nc.scalar.activation(kbt[c][:, h * 256:(h + 1) * 256], bkb[:, h * 256:(h + 1) * 256],
                                                    AF.Identity, scale=kdb[:, h * 256:h * 256 + 1]),
                       reads=[bR, R_kd], writes=[kbt_R[c]])
            for h in range(4):
                ACT.op(lambda: nc.scalar.activation(kft[c][:, h * 256:(h + 1) * 256], bkb[:, h * 256:(h + 1) * 256],
                                                    AF.Identity, scale=kdf[:, h * 256:h * 256 + 1]),
                       reads=[bR, R_kd], writes=[kft_R[c]])
            STAGE[0] = prev

        def kT_tail(t):
            kT_chunk(t, 2)
            kT_chunk(t, 3)
            ACT.dma(c_kfst, kf_scr[4 * t:4 * t + 4].rearrange("c p f -> p c f"),
                   kvA[:, 8:16, :].rearrange("p (c a) b -> p c (a b)", c=4), reads=kft_R, writes=[kfs_R[t]])

        def s9_chunk(t, c):
            prev = STAGE[0]
            STAGE[0] = "P1.S9state"
            n = 4 * t + c
            cur = sbf["cur"]
            ACT.dma(c_sbf[cur], sb_scr[n], Sbf[:, cur], reads=Sbf_R[cur], writes=[sbs_R[n]])
            sbf["cur"] = cur ^ 1
            state_update(n, kbt[c], kbt_R[c], vtm[:, c, :], vtm_R[c], GCB, cur ^ 1, boundary=(n == 16))
            STAGE[0] = prev

        def s9(t):
            for c in range(3, -1, -1):
                s9_chunk(t, c)

        def issue_xin(t):
            for c in range(4):
                POOL.dma(c_xin[c], xin[:, c, :], x_in[t * T + c * 128:t * T + (c + 1) * 128, :], writes=[xin_R[c]])
        issue_xin(NT - 1)
        gen_consts()
        gen_cs(NT - 1)
        for t in range(NT - 1, -1, -1):
            STAGE[0] = "P1.S1in"
            for k in range(8):
                bk, bR = bank()
                for c in range(4):
                    PE.op(lambda: nc.tensor.transpose(bk[:, c * 128:(c + 1) * 128],
                                                      xin[:, c, k * 128:(k + 1) * 128], ident[:]),
                          reads=[xin_R[c], R_id], writes=[bR], signal=(c == 3))
                if k % 2 == 0:
                    ACT.op(lambda: nc.scalar.copy(xT_k[k], bk[:]), reads=[bR], writes=[xT_R[k]])
                else:
                    DVE.op(lambda: nc.vector.tensor_copy(xT_k[k], bk[:]), reads=[bR], writes=[xT_R[k]])
            if t > 0:
                issue_xin(t - 1)
            STAGE[0] = "P1.N1"
            rmsnorm(T, xT_k, xT_R, g_mix0, xn_k, xn_R, rstd[:, 0, :], rstd_R[0])
            if t < NT - 1:
                kT_tail(t + 1)
            STAGE[0] = "P1.S3win"
            for c in range(8):
                slab, sR = w_next(("win", c))
                if t == NT - 1:
                    edge_chunk(c, slab, sR)
                bs = [bank() for _ in range(3)]
                if c == 0:
                    keep_warm(*bs[0])
                for j in (0, 2, 1):
                    bk_, bR_ = bs[j]
                    for k in range(8):
                        PE.op(lambda: nc.tensor.matmul(bk_[:], slab[:, k * 384 + j * 128:k * 384 + (j + 1) * 128],
                                                       xn_k[k], start=(k == 0), stop=(k == 7)),
                              reads=[sR, xn_R[k]], writes=[bR_], signal=(k == 7))
                (bh, bhR), (bgb, bgbR), (bgc, bgcR) = bs
                hb = c % 2
                ACT.op(lambda: nc.scalar.copy(htmp[hb], bh[:]), reads=[bhR], writes=htmp_R[hb])
                DVE.op(lambda: nc.vector.tensor_tensor(u_k[c], bgc[:], htmp[hb], ALU.mult),
                       reads=[bgcR] + htmp_R[hb], writes=u_R[c])
                z = ztmp[hb]; zR = ztmp_R[hb]
                ACT.op(lambda: nc.scalar.activation(z, u_k[c], AF.Identity, bias=cb(c), scale=cw(1, c)),
                       reads=u_R[c] + [R_const], writes=zR)
                DVE.op(lambda: nc.vector.scalar_tensor_tensor(z[:, 1:T], u_k[c][:, 0:T - 1], cw(0, c), z[:, 1:T],
                                                               ALU.mult, ALU.add),
                        reads=u_R[c] + zR + [R_const], writes=zR)
                DVE.op(lambda: nc.vector.scalar_tensor_tensor(z[:, 0:T - 1], u_k[c][:, 1:T], cw(2, c), z[:, 0:T - 1],
                                                               ALU.mult, ALU.add),
                        reads=u_R[c] + zR + [R_const], writes=zR)
                if t >= 1:
                    e = 2 * (t - 1)
                    DVE.op(lambda: nc.vector.scalar_tensor_tensor(z[:, 0:1], uedge[:, c, e:e + 1], cw(0, c), z[:, 0:1],
                                                                   ALU.mult, ALU.add),
                            reads=[R_uec[c], R_const] + zR, writes=zR)
                if t <= NT - 2:
                    e = 2 * t + 1
                    DVE.op(lambda: nc.vector.scalar_tensor_tensor(z[:, T - 1:T], uedge[:, c, e:e + 1], cw(2, c),
                                                                   z[:, T - 1:T], ALU.mult, ALU.add),
                            reads=[R_uec[c], R_const] + zR, writes=zR)
                DVE.op(lambda: nc.vector.tensor_tensor(gz_k[c], bgb[:], z, ALU.mult),
                       reads=[bgbR] + zR, writes=[gz_R[c]])
            STAGE[0] = "P1.S4wout"
            for i in range(2):
                slab, sR = w_next(("wout", i))

                def evac(mm, bk, bR, i=i):
                    m = i * 4 + mm
                    DVE.op(lambda: nc.vector.tensor_tensor(xT_k[m], bk[:], xT_k[m], ALU.add),
                           reads=[bR, xT_R[m]], writes=[xT_R[m]])
                proj_fm(slab, sR, 8, 4, gz_k, gz_R, evac)
            STAGE[0] = "P1.mlp"
            mlp("up0", "dn0", g_mlp0,
                slab_hook=(lambda i, t=t: s9_chunk(t + 1, 3 - i // 2) if i % 2 == 0 else None) if t < NT - 1 else None)
            STAGE[0] = "P1.N3"
            if t > 0:
                gen_cs_iota(t - 1)
            POOL.dma(c_x1st, x1_scr[t], xT[:], reads=xT_R, writes=[x1_R[t]])
            rmsnorm(T, xT_k, xT_R, g_mix1, xn_k, xn_R, rstd[:, 0, :], rstd_R[0])
            STAGE[0] = "P1.S7k"
            for i in range(2):
                slab, sR = w_next(("wk", i))
                pend = []

                def evac(mm, bk, bR, i=i, pend=pend):
                    pend.append((bk, bR))
                    if len(pend) == 2:
                        (b1, b1R), (b2, b2R) = pend
                        m0 = i * 4 + (mm - 1)

                        def e1(r, rR, m0=m0):
                            ACT.op(lambda: nc.scalar.copy(kfm[m0], r), reads=[rR], writes=[kfm_R[m0]])

                        def e2(r, rR, m0=m0):
                            ACT.op(lambda: nc.scalar.copy(kfm[m0 + 1], r), reads=[rR], writes=[kfm_R[m0 + 1]])
                        rotary(b1, b1R, b2, b2R, e1, e2)
                        pend.clear()
                proj_fm(slab, sR, 8, 4, xn_k, xn_R, evac, warm=(i == 0))
            ACT.dma(c_kst, k_scr[t], kvA[:, 0:8, :], reads=kfm_R, writes=[ks_R[t]])
            STAGE[0] = "P1.S8v"
            for h in range(4):
                slab, sR = w_next(("wv", h))
                if h in (1, 2):
                    kT_chunk(t, h - 1)
                if h == 2 and t > 0:
                    gen_cs(t - 1, do_iota=False)
                for c in range(4):
                    bk, bR = bank()
                    for k in range(8):
                        PE.op(lambda: nc.tensor.matmul(bk[:], xn_k[k][:, c * 128:(c + 1) * 128],
                                                       slab[:, k * 512:(k + 1) * 512], start=(k == 0), stop=(k == 7)),
                              reads=[sR, xn_R[k]], writes=[bR], signal=(k == 7))
                    ACT.op(lambda: nc.scalar.copy(vtm[:, c, h * 512:(h + 1) * 512], bk[:]),
                           reads=[bR], writes=[vtm_R[c][h]])
            ACT.dma(c_vst, v_scr[4 * t:4 * t + 4].rearrange("c p f -> p c f"), vtm[:],
                   reads=[r for c in range(4) for r in vtm_R[c]], writes=[vs_R[t]])
            STAGE[0] = "P1.S9state"

        kT_tail(0)
        s9(0)

        q_k = [hm[:, m, :] for m in range(8)]; q_R = [HR[m] for m in range(8)]
        qf_k = [hm[:, 8 + m, :] for m in range(8)]; qf_R = [HR[8 + m] for m in range(8)]
        qb_k = [hm[:, 16 + m, :] for m in range(8)]; qb_R = [HR[16 + m] for m in range(8)]
        k2 = [hm[:, 24 + m, :] for m in range(8)]; k2_R = [HR[24 + m] for m in range(8)]
        c_k2 = chan("c_k2")
        gact = vtm
        gact_R = vtm_R
        gT = xin[:].rearrange("p c f -> p (c f)").bitcast(BF16).rearrange("p (j n) -> p j n", j=16)
        gT_R = [Region(f"gT{j}") for j in range(16)]
        kf2s = [ktm[:], kvA[:, 22:24, :].rearrange("p a b -> p (a b)")]
        kf2s_R = [ktm_R, Region("kf2b")]
        c_kf2 = [chan("c_kf2a"), chan("c_kf2b")]

        def ret_loads_kv(n):
            b = n % 2
            SP.dma(c_kf2[b], kf2s[b], kf_scr[n], reads=[kfs_R[n // 4]], writes=[kf2s_R[b]])
            SP.dma(c_v2[b], v2s[b], v_scr[n], reads=[vs_R[n // 4]], writes=[r for h in range(4) for r in v2s_R[b][h]])

        def ret_loads_sb(n):
            for h in range(4):
                SP.dma(c_sbl[h], sbl[:, 2 * h:2 * h + 2, :], sb_scr[n][:, 2 * h:2 * h + 2, :],
                       reads=[sbs_R[n]], writes=[sbl_R[h]])
        v2s = [kdf[:].bitcast(BF16),
               rot[:, 2:4, :].rearrange("p a b -> p (a b)").bitcast(BF16)]
        v2s_R = [[[Region(f"v2_{h}")] for h in range(4)], [[rot_R[2], rot_R[3]] for h in range(4)]]
        c_v2 = [chan("c_v2a"), chan("c_v2b")]
        sbl = kvA[:, 0:8, :]
        sbl_R = [Region(f"sbl{h}") for h in range(4)]
        c_sbl = [chan(f"c_sbl{h}") for h in range(4)]
        PT = kvA[:, 8:10, :]
        PT_R = [Region("PT0"), Region("PT1")]
        gated = kvA[:, 10:18, :]
        gated_R = [[Region(f"gated{b}_{h}") for h in range(4)] for b in range(2)]
        on_ = [rot[:, 0, :], rot[:, 1, :]]
        youts = [rstd[:, 1:3, :].rearrange("p a b -> p (a b)"),
                 rot[:, 0:2, :].rearrange("p a b -> p (a b)")]
        youts_R = [[rstd_R[1], rstd_R[2]], [rot_R[0], rot_R[1]]]
        c_yout = [chan("c_yout0"), chan("c_yout1")]
        c_x1ld = [chan(f"c_x1ld{k}") for k in range(8)]

        def p2_prefetch_k(t):
            POOL.dma(c_k2, hm[:, 24:32, :], k_scr[t], reads=[ks_R[t]], writes=k2_R)
        out_toks = []

        engs = [PE, ACT, DVE, POOL]
        Rbar = Region("bar")
        STAGE[0] = "barrier"
        for e in engs:
            if e is PE:
                e.op(lambda: nc.tensor.matmul(banks[0][:, 0:16], onesb[:], onesb[:, 0:16], start=True, stop=True),
                     reads=[R_const], writes=[bank_R[0]])
        allR = (kfm_R + kft_R + kbt_R + [ktm_R] + [r for c in range(4) for r in vtm_R[c]] + rot_R + HR +
                Sbf_R[0] + Sbf_R[1] + S_R + xin_R + [R_kd])
        for e in (ACT, DVE, POOL, SP, PE):
            e._deps([], allR + bank_R)
        for R in allR:
            R.w = None
            R.r = {}
        POOL.op(lambda: nc.gpsimd.memset(S[:], 0.0), writes=S_R)
        POOL.op(lambda: nc.gpsimd.memset(Sbf[:, 0], 0.0), writes=Sbf_R[0])
        SfB = Sbf_R[0]

        def f32v(ap3):
            return ap3.rearrange("p a b -> p (a b)").bitcast(F32)
        xTb_k = ([f32v(Sbf[:, 1, 2 * j:2 * j + 2, :]) for j in range(4)] +
                 [f32v(kvA[:, 18 + 2 * j:20 + 2 * j, :]) for j in range(2)] +
                 [kdb[:, 0:512], kdb[:, 512:1024]])
        xTb_R = [Region(f"xTb{k}") for k in range(8)]
        XT = [(xT_k, xT_R), (xTb_k, xTb_R)]

        def p2_prefetch(t):
            XK, XR = XT[t % 2]
            for k in range(8):
                POOL.dma(c_x1ld[k], XK[k], x1_scr[t][:, k, :], reads=[x1_R[t]], writes=[XR[k]])

        rotb = rot[:].rearrange("p a b -> p (a b)").bitcast(BF16)
        sqn_k = [rotb[:, j * T:(j + 1) * T] for j in range(8)]
        sqn_R = [rot_R[j // 2] for j in range(8)]

        def p2_n1a(t):
            XK, XR = XT[t % 2]
            rms_sq(XK, XR, sqn_k, sqn_R)

        def p2_n1b(t):
            prev = STAGE[0]
            STAGE[0] = "P2.N1"
            XK, XR = XT[t % 2]
            rmsnorm(T, XK, XR, g_mix1, xn_k, xn_R, rstd[:, 0, :], rstd_R[0], sq=sqn_k, sq_R=sqn_R, do_sq=False)
            STAGE[0] = prev

        def p2_n1(t):
            p2_n1a(t)
            p2_n1b(t)

        def p2_q(t, slabs=(0, 1), deferred=None):
            prev = STAGE[0]
            STAGE[0] = "P2.q"
            for i in slabs:
                slab, sR = w_next(("wq", i))
                pend = []

                def evac(mm, bk, bR, i=i, pend=pend):
                    pend.append((bk, bR))
                    if len(pend) == 2:
                        (b1, b1R), (b2, b2R) = pend
                        m0 = i * 4 + (mm - 1)
                        h = m0 // 2

                        def mk(m):
                            def e(r, rR):
                                ACT.op(lambda: nc.scalar.copy(q_k[m], r), reads=[rR], writes=[q_R[m]])
                                r3 = r.rearrange("p (c i) -> p c i", c=4)
                                POOL.op(lambda: nc.gpsimd.tensor_tensor(
                                    qf_k[m].rearrange("p (c i) -> p c i", c=4), r3,
                                    decqf[:, h:h + 1, :].to_broadcast([128, 4, 128]), ALU.mult),
                                    reads=[rR, R_const], writes=[qf_R[m]])
                                POOL.op(lambda: nc.gpsimd.tensor_tensor(
                                    qb_k[m].rearrange("p (c i) -> p c i", c=4), r3,
                                    decqb[:, h:h + 1, :].to_broadcast([128, 4, 128]), ALU.mult),
                                    reads=[rR, R_const], writes=[qb_R[m]])
                            return e

                        def run(b1=b1, b1R=b1R, b2=b2, b2R=b2R, m0=m0):
                            rotary(b1, b1R, b2, b2R, mk(m0), mk(m0 + 1), comb="dve")
                        if deferred is None:
                            run()
                        else:
                            deferred.append(run)
                        pend.clear()
                proj_fm(slab, sR, 8, 4, xn_k, xn_R, evac)
            STAGE[0] = prev

        p2_prefetch(0)
        load_cs(0)
        p2_prefetch_k(0)
        ret_loads_kv(0)
        ret_loads_sb(0)
        p2_n1(0)
        p2_q(0)
        for t in range(NT):
            XK, XR = XT[t % 2]
            STAGE[0] = "P2.g"
            for gi in range(4):
                slab, sR = w_next(("wg", gi))
                for c in range(4):
                    bk, bR = bank()
                    for k in range(8):
                        PE.op(lambda: nc.tensor.matmul(bk[:], xn_k[k][:, c * 128:(c + 1) * 128],
                                                       slab[:, k * 512:(k + 1) * 512], start=(k == 0), stop=(k == 7)),
                              reads=[sR, xn_R[k]], writes=[bR], signal=(k == 7))
                    ACT.op(lambda: nc.scalar.activation(gact[:, c, gi * 512:(gi + 1) * 512], bk[:], AF.Silu),
                           reads=[bR], writes=[gact_R[c][gi]])
            act_preload(AF.Ln)
            if t + 1 < NT:
                p2_prefetch(t + 1)
                load_cs(t + 1)
            STAGE[0] = "P2.ret"
            for c in range(4):
                n = 4 * t + c
                cs_ = slice(c * 128, (c + 1) * 128)
                vb = n % 2
                v2 = v2s[vb]; v2_R = v2s_R[vb]; kf2 = kf2s[vb]; kf2_R = kf2s_R[vb]
                if n + 1 < NCH:
                    ret_loads_kv(n + 1)
                def emit_scores(cc):
                    prev = STAGE[0]
                    STAGE[0] = "P2.ret.sc"
                    csl = slice(cc * 128, (cc + 1) * 128)
                    bs_, bsR = bank("r")
                    for hh in range(4):
                        for dc in range(2):
                            m = 2 * hh + dc
                            PE.op(lambda: nc.tensor.matmul(bs_[:, hh * 128:(hh + 1) * 128], k2[m][:, csl], q_k[m][:, csl],
                                                           start=(dc == 0), stop=(dc == 1)),
                                  reads=[k2_R[m], q_R[m]], writes=[bsR], signal=(hh == 3 and dc == 1))
                    DVE.op(lambda: nc.vector.tensor_tensor(PT[:, cc % 2, :], bs_[:], DT[:].rearrange("p h i -> p (h i)"),
                                                           ALU.mult),
                           reads=[bsR, R_const], writes=[PT_R[cc % 2]])
                    STAGE[0] = prev

                if c == 0:
                    emit_scores(0)
                pb = c % 2
                PTb = PT[:, pb, :]
                gb_ = c % 2
                STAGE[0] = "P2.ret.o"
                obanks = []

                def gn_finish(h, c=c, gb_=gb_, obanks=obanks):
                    bo, boR = obanks[h]
                    ag = misc[:, 32 + 4 * h:32 + 4 * h + 2]
                    rs_ = misc[:, 32 + 4 * h + 2:32 + 4 * h + 3]; nm = misc[:, 32 + 4 * h + 3:32 + 4 * h + 4]
                    mR = misc_R[h]
                    DVE.op(lambda: nc.vector.scalar_tensor_tensor(nm, ag[:, 0:1], -1.0, rs_, ALU.mult, ALU.mult),
                           reads=[mR], writes=[mR])
                    ob = h % 2
                    ACT.op(lambda: nc.scalar.activation(on_[ob], bo[:], AF.Identity, bias=nm, scale=rs_),
                           reads=[boR, mR], writes=[rot_R[ob]])
                    POOL.op(lambda: nc.gpsimd.tensor_tensor(gated[:, 4 * gb_ + h, :], on_[ob],
                                                            gact[:, c, h * 512:(h + 1) * 512], ALU.mult),
                            reads=[rot_R[ob], gact_R[c][h]], writes=[gated_R[gb_][h]])

                for h in range(4):
                    bo, boR = bank("o")
                    PE.op(lambda: nc.tensor.matmul(bo[:], PTb[:, h * 128:(h + 1) * 128], v2[:, h * 512:(h + 1) * 512],
                                                   start=True, stop=False),
                          reads=[PT_R[pb]] + v2_R[h], writes=[boR], signal=False)
                    for dc in range(2):
                        m = 2 * h + dc
                        PE.op(lambda: nc.tensor.matmul(bo[:], qf_k[m][:, cs_], Sbf[:, 0, m, :], start=False, stop=False),
                              reads=[qf_R[m], SfB[m]], writes=[boR], signal=False)
                    for dc in range(2):
                        m = 2 * h + dc
                        PE.op(lambda: nc.tensor.matmul(bo[:], qb_k[m][:, cs_], sbl[:, m, :], start=False, stop=(dc == 1)),
                              reads=[qb_R[m], sbl_R[h]], writes=[boR], signal=(dc == 1))
                    st = misc[:, 8 * h:8 * h + 6]; ag = misc[:, 32 + 4 * h:32 + 4 * h + 2]
                    rs_ = misc[:, 32 + 4 * h + 2:32 + 4 * h + 3]
                    mR = misc_R[h]
                    DVE.op(lambda: nc.vector.bn_stats(st, bo[:]), reads=[boR], writes=[mR])
                    DVE.op(lambda: nc.vector.bn_aggr(ag, st), reads=[mR], writes=[mR])
                    ACT.op(lambda: nc.scalar.activation(rs_, ag[:, 1:2], AF.Ln, bias=EPS, scale=1.0), reads=[mR], writes=[mR])
                    ACT.op(lambda: nc.scalar.activation(rs_, rs_, AF.Exp, scale=-0.5), reads=[mR], writes=[mR])
                    obanks.append((bo, boR))
                    if h >= 1:
                        gn_finish(h - 1)
                    if h >= 2:
                        STAGE[0] = "P2.ret.st"
                        state_update(n, kf2, kf2_R, v2, v2_R, GCF, 0, boundary=(n == 15), grp="r", heads=(h - 2,))
                        STAGE[0] = "P2.ret.o"
                gn_finish(3)
                if c < 3:
                    emit_scores(c + 1)
                if n + 1 < NCH:
                    ret_loads_sb(n + 1)
                STAGE[0] = "P2.ret.st"
                state_update(n, kf2, kf2_R, v2, v2_R, GCF, 0, boundary=(n == 15), grp="r", heads=(2, 3))
                STAGE[0] = "P2.ret.gT"
                for half in range(2):
                    bk, bR = bank("r")
                    bkb = bk[:].bitcast(BF16)
                    for j in range(8):
                        kc = half * 8 + j
                        hh = kc // 4
                        PE.op(lambda: nc.tensor.transpose(bkb[:, j * 128:(j + 1) * 128],
                                                          gated[:, 4 * gb_ + hh, (kc % 4) * 128:(kc % 4 + 1) * 128],
                                                          identb[:]),
                              reads=[gated_R[gb_][hh], R_const], writes=[bR], signal=(j == 7))
                    ACT.op(lambda: nc.scalar.copy(gT[:, half * 8:half * 8 + 8, cs_],
                                                  bkb.rearrange("p (j i) -> p j i", j=8)),
                           reads=[bR], writes=gT_R[half * 8:half * 8 + 8])
            if t + 1 < NT:
                p2_n1a(t + 1)
            STAGE[0] = "P2.wo"
            gT_k = [gT[:, j, :] for j in range(16)]
            for i in range(4):
                slab, sR = w_next(("wo", i))

                def evac(mm, bk, bR, i=i):
                    m = i * 2 + mm
                    DVE.op(lambda: nc.vector.tensor_tensor(XK[m], bk[:], XK[m], ALU.add),
                           reads=[bR, XR[m]], writes=[XR[m]])
                proj_fm(slab, sR, 16, 2, gT_k, gT_R, evac, warm=(i == 0))
            STAGE[0] = "P2.mlp"
            mlp("up1", "dn1", g_mlp1, XK, XR, mid_hook=(lambda: p2_n1b(t + 1)) if t + 1 < NT else None)
            if t + 1 < NT:
                p2_prefetch_k(t + 1)
            STAGE[0] = "P2.N3out"
            rmsnorm(T, XK, XR, g_fin, XK, XR, rstd[:, 0, :], rstd_R[0],
                    sq=[hm[:, k, :] for k in range(8)], sq_R=[HR[k] for k in range(8)])
            dq = []
            if t + 1 < NT:
                p2_q(t + 1, slabs=(0,), deferred=dq)
            STAGE[0] = "P2.out"
            for c in range(4):
                for half in range(2):
                    bk, bR = bank("side")
                    for j in range(4):
                        k = half * 4 + j
                        PE.op(lambda: nc.tensor.transpose(bk[:, j * 128:(j + 1) * 128],
                                                          XK[k][:, c * 128:(c + 1) * 128], ident[:]),
                              reads=[XR[k], R_id], writes=[bR], signal=(j == 3))
                    yb = c % 2
                    if half == 0:
                        ACT.op(lambda: nc.scalar.copy(youts[yb][:, 0:512], bk[:]), reads=[bR], writes=[youts_R[yb][0]])
                    else:
                        DVE.op(lambda: nc.vector.tensor_copy(youts[yb][:, 512:1024], bk[:]), reads=[bR],
                               writes=[youts_R[yb][1]])
                r0 = t * T + c * 128
                tok = SP.dma(c_yout[yb], y_out[r0:r0 + 128, :], youts[yb], reads=youts_R[yb])
                out_toks.append(tok)
            for run in dq:
                run()
            if t + 1 < NT:
                p2_q(t + 1, slabs=(1,))
                act_preload(AF.Silu)

        for tok in out_toks[-2:]:
            SP._wait(tok)
        assert wst["next"] == len(seq)
    return nc


_NC_CACHE = {}


def _small_pack(inp):
    def fm(v):
        return np.ascontiguousarray(np.asarray(v, np.float32).reshape(8, 128).T)
    cols = [fm(inp["norm_mix_0"]), fm(inp["norm_mlp_0"]), fm(inp["norm_mix_1"]), fm(inp["norm_mlp_1"]),
            fm(inp["norm_final"])]
    cwv = np.asarray(inp["conv_w_0"], np.float32)
    for j in range(3):
        cols.append(fm(cwv[j]))
    cols.append(fm(inp["conv_b_0"]))
    return np.ascontiguousarray(np.concatenate(cols, axis=1))


def kernel(**inputs):
    if "nc" not in _NC_CACHE:
        _NC_CACHE["nc"] = build_nc()
    nc = _NC_CACHE["nc"]
    xp = np.asarray(inputs["x_prompt"], np.float32)
    xs = np.asarray(inputs["x_sample"], np.float32)
    small = _small_pack(inputs)
    half = 128
    rope_inv = (np.float32(10000.0) ** (-np.arange(half, dtype=np.float32) / np.float32(half))).astype(np.float32)
    rope_inv = np.ascontiguousarray(rope_inv.reshape(128, 1))
    wnames = ["w_in_conv_0", "w_out_conv_0", "w_up_0", "w_down_0", "w_qkvg_1", "w_o_1", "w_up_1", "w_down_1"]
    shared = {n: np.ascontiguousarray(np.asarray(inputs[n], np.float32)) for n in wnames}
    in_maps = []
    for c in range(8):
        if c < 4:
            x = xp[c]
            flag = np.tile(np.array([[1.0, 0.0]], np.float32), (128, 1))
        else:
            x = xs[2 * (c - 4):2 * (c - 4) + 2].reshape(NTOK, D)
            flag = np.tile(np.array([[0.0, 2048.0]], np.float32), (128, 1))
        m = dict(shared)
        m["x"] = np.ascontiguousarray(x)
        m["small"] = small
        m["flag"] = np.ascontiguousarray(flag)
        m["rope_inv"] = rope_inv
        in_maps.append(m)
    res = run_bass_kernel_spmd(nc, in_maps, core_ids=list(range(8)))
    ys = [np.asarray(r["y"], np.float32) for r in res.results]
    y_prompt = np.stack(ys[0:4], axis=0)
    y_sample = np.stack([ys[4 + i].reshape(2, 2048, D) for i in range(4)], axis=0).reshape(8, 2048, D)
    if DEBUG:
        kernel.debug = res.results
    return (y_prompt, y_sample)
```

```python
import contextlib
import math
import numpy as np
import concourse.bass as bass
import concourse.mybir as mybir
from concourse.bass_utils import run_bass_kernel_spmd

F32 = mybir.dt.float32
BF16 = mybir.dt.bfloat16
AF = mybir.ActivationFunctionType
ALU = mybir.AluOpType

D = 1024
NTOK = 4096
T = 512
NT = NTOK // T
NCH = NTOK // 128
EPS = 1e-6
NSLOT = 3
LGF = [math.log(1.0 - 2.0 ** (-5.0 - h)) for h in range(4)]
LGB = [math.log(1.0 - 2.0 ** (-5.5 - h)) for h in range(4)]
QS = 256 ** -0.5
PI = math.pi
PI_LO = 3.1415925

DEBUG = False
STAGE = ["init"]
PE_LABELS = []


class Region:
    __slots__ = ("name", "w", "r", "psum")

    def __init__(self, name, psum=False):
        self.name = name
        self.w = None
        self.r = {}
        self.psum = psum


class Chan:
    def __init__(self, sem):
        self.sem = sem
        self.count = 0


class Eng:
    def __init__(self, h, sem, is_pe=False):
        self.h = h
        self.sem = sem
        self.cnt = 0
        self.waited = {}
        self.is_pe = is_pe

    def _wait(self, tok):
        sem, val = tok
        if sem is self.sem and (self.is_pe or self.sem is None):
            return
        key = id(sem)
        if self.waited.get(key, 0) >= val:
            return
        self.h.wait_ge(sem, val)
        self.waited[key] = val

    def _deps(self, reads, writes):
        toks = {}
        for R in reads:
            if R.w is not None:
                toks[(id(R.w[0]), R.w[1])] = R.w
            if R.psum:
                for t in R.r.values():
                    if t[0] is not self.sem:
                        toks[(id(t[0]), t[1])] = t
        for R in writes:
            if R.w is not None:
                toks[(id(R.w[0]), R.w[1])] = R.w
            for t in R.r.values():
                toks[(id(t[0]), t[1])] = t
        for t in toks.values():
            self._wait(t)

    @staticmethod
    def _mark(reads, writes, tok):
        k = id(tok[0])
        for R in reads:
            old = R.r.get(k)
            if old is None or old[1] < tok[1]:
                R.r[k] = tok
        for R in writes:
            R.w = tok
            R.r = {}

    def op(self, fn, reads=(), writes=(), signal=True):
        self._deps(reads, writes)
        inst = fn()
        if self.is_pe:
            PE_LABELS.append(STAGE[0])
        if signal:
            self.cnt += 1
            inst.then_inc(self.sem, 1)
            tok = (self.sem, self.cnt)
        else:
            tok = (self.sem, self.cnt + 1)
        self._mark(reads, writes, tok)
        return tok

    def dma(self, chan, out, in_, reads=(), writes=(), n=1, **kw):
        self._deps(reads, writes)
        if n == 1:
            outs, ins = [out], [in_]
        else:
            outs, ins = out, in_
        for o, i in zip(outs, ins):
            self.h.dma_start(out=o, in_=i, **kw).then_inc(chan.sem, 16)
            chan.count += 16
        tok = (chan.sem, chan.count)
        self._mark(reads, writes, tok)
        return tok


def build_nc():
    nc = bass.Bass("TRN2", target_bir_lowering=False)
    es = contextlib.ExitStack()

    def dram(name, shape, dt, kind):
        return nc.dram_tensor(name, shape, dt, kind=kind)

    x_in = dram("x", [NTOK, D], F32, "ExternalInput").ap()
    w_in_d = dram("w_in_conv_0", [D, 3 * D], F32, "ExternalInput").ap()
    w_outc_d = dram("w_out_conv_0", [D, D], F32, "ExternalInput").ap()
    w_up0_d = dram("w_up_0", [D, 4 * D], F32, "ExternalInput").ap()
    w_dn0_d = dram("w_down_0", [4 * D, D], F32, "ExternalInput").ap()
    w_qkvg_d = dram("w_qkvg_1", [D, 6 * D], F32, "ExternalInput").ap()
    w_o_d = dram("w_o_1", [2 * D, D], F32, "ExternalInput").ap()
    w_up1_d = dram("w_up_1", [D, 4 * D], F32, "ExternalInput").ap()
    w_dn1_d = dram("w_down_1", [4 * D, D], F32, "ExternalInput").ap()
    small_d = dram("small", [128, 72], F32, "ExternalInput").ap()
    flag_d = dram("flag", [128, 2], F32, "ExternalInput").ap()
    inv_d = dram("rope_inv", [128, 1], F32, "ExternalInput").ap()
    y_out = dram("y", [NTOK, D], F32, "ExternalOutput").ap()

    skind = "ExternalOutput" if DEBUG else "Internal"
    NSLAB = 58
    wscr = dram("wscr", [NSLAB, 128, 4096], BF16, "Internal").ap()
    x1_scr = dram("x1_scr", [NT, 128, 8, T], F32, skind).ap()
    cs_scr = dram("cs_scr", [NT, 128, 2, T], F32, skind).ap()
    k_scr = dram("k_scr", [NT, 128, 8, T], BF16, skind).ap()
    kf_scr = dram("kf_scr", [NCH, 128, 1024], BF16, skind).ap()
    v_scr = dram("v_scr", [NCH, 128, 2048], BF16, skind).ap()
    sb_scr = dram("sb_scr", [NCH, 128, 8, T], BF16, skind).ap()

    with es:
        def sb(name, shape, dt):
            return es.enter_context(nc.sbuf_tensor("sb_" + name, shape, dt))

        def new_sem(name):
            return es.enter_context(nc.semaphore(name))

        def chan(name):
            return Chan(new_sem(name))

        PE = Eng(nc.tensor, new_sem("s_pe"), is_pe=True)
        ACT = Eng(nc.scalar, new_sem("s_act"))
        DVE = Eng(nc.vector, new_sem("s_dve"))
        POOL = Eng(nc.gpsimd, new_sem("s_pool"))
        SP = Eng(nc.sync, None)

        wring = sb("wring", [128, NSLOT, 4096], BF16)
        slot_R = [Region(f"slot{i}") for i in range(NSLOT)]
        slot_ch = [chan(f"c_slot{i}") for i in range(NSLOT)]

        ident = sb("ident", [128, 128], F32)
        identb = sb("identb", [128, 128], BF16)
        onesb = sb("onesb", [128, 128], BF16)
        DT = sb("DT", [128, 4, 128], F32)
        decqf = sb("decqf", [128, 4, 128], F32)
        decqb = sb("decqb", [128, 4, 128], F32)
        kdf = sb("kdf", [128, 1024], F32)
        kdb = sb("kdb", [128, 1024], F32)
        small = sb("small", [128, 72], F32)
        flagt = sb("flagt", [128, 2], F32)
        invt = sb("invt", [128, 1], F32)
        R_const = Region("const")
        R_kd = Region("kd")

        xT = sb("xT", [128, 8, T], F32)
        xT_R = [Region(f"xT{k}") for k in range(8)]
        xn = sb("xn", [128, 8, T], BF16)
        xn_R = [Region(f"xn{k}") for k in range(8)]
        rstd = sb("rstd", [128, 3, T], F32)
        rstd_R = [Region("rstd0"), Region("rstd1"), Region("rstd2")]
        cst = sb("cst", [128, 2, T], F32)
        cst_R = Region("cst")
        c_cst = chan("c_cst")
        S = sb("S", [128, 8, T], F32)
        S_R = [Region(f"S{i}") for i in range(8)]
        Sbf = sb("Sbf", [128, 2, 8, T], BF16)
        Sbf_R = [[Region(f"Sbf{b}_{i}") for i in range(8)] for b in range(2)]
        c_sbf = [chan("c_sbf0"), chan("c_sbf1")]

        hm = sb("hm", [128, 32, T], BF16)
        HR = [Region(f"hm{j}") for j in range(32)]

        def hm_f32(j):
            return hm[:, 2 * j:2 * j + 2, :].rearrange("p a b -> p (a b)").bitcast(F32)

        misc = sb("misc", [128, 64], F32)
        misc_R = [Region(f"misc{i}") for i in range(16)]
        R_tbl = Region("tblswitch")

        def act_preload(func):
            ACT.op(lambda: nc.scalar.activation(misc[:, 62:63], misc[:, 60:61], func, bias=1.0),
                   reads=[R_tbl], writes=[R_tbl])

        banks = [es.enter_context(nc.psum_tensor(f"bank{i}", [128, T], F32)) for i in range(8)]
        bank_R = [Region(f"bank{i}", psum=True) for i in range(8)]
        BG = {"main": [0, 1, 2, 3, 4, 5], "side": [6, 7], "o": [0, 1, 2, 3], "r": [4, 5, 6, 7]}
        bstate = {g: 0 for g in BG}

        def bank(g="main"):
            lst = BG[g]
            i = lst[bstate[g] % len(lst)]
            bstate[g] += 1
            return banks[i], bank_R[i]

        slabs = []

        def add_cols(name, W, K, c0, idx):
            KC = K // 128
            C = 4096 // KC
            src = W[:, c0:c0 + C].rearrange("(kc p) c -> p kc c", p=128)
            slabs.append(dict(name=f"{name}{idx}", KC=KC, C=C, used=4096, src=src, kind="cols"))

        SL = {}
        for c in range(8):
            src = w_in_d.rearrange("(kc p) (g f) -> p kc g f", p=128, g=3)[:, :, :, c * 128:(c + 1) * 128]
            SL[("win", c)] = len(slabs)
            slabs.append(dict(name=f"win{c}", KC=8, C=384, used=3072, src=src, kind="win"))
        for i in range(2):
            SL[("wout", i)] = len(slabs); add_cols("wout", w_outc_d, D, i * 512, i)
        for i in range(8):
            SL[("up0", i)] = len(slabs); add_cols("up0", w_up0_d, D, i * 512, i)
        for i in range(8):
            SL[("dn0", i)] = len(slabs); add_cols("dn0", w_dn0_d, 4 * D, i * 128, i)
        for i in range(2):
            SL[("wk", i)] = len(slabs); add_cols("wk", w_qkvg_d, D, 1024 + i * 512, i)
        for i in range(4):
            SL[("wv", i)] = len(slabs); add_cols("wv", w_qkvg_d, D, 2048 + i * 512, i)
        for i in range(2):
            SL[("wq", i)] = len(slabs); add_cols("wq", w_qkvg_d, D, i * 512, i)
        for i in range(4):
            SL[("wg", i)] = len(slabs); add_cols("wg", w_qkvg_d, D, 4096 + i * 512, i)
        for i in range(4):
            SL[("wo", i)] = len(slabs); add_cols("wo", w_o_d, 2 * D, i * 256, i)
        for i in range(8):
            SL[("up1", i)] = len(slabs); add_cols("up1", w_up1_d, D, i * 512, i)
        for i in range(8):
            SL[("dn1", i)] = len(slabs); add_cols("dn1", w_dn1_d, 4 * D, i * 128, i)
        assert len(slabs) == NSLAB
        slab_R = [Region(f"wscr{i}") for i in range(NSLAB)]

        p1_list = ([("win", c) for c in range(8)] + [("wout", i) for i in range(2)] +
                   [("up0", i) for i in range(8)] + [("dn0", i) for i in range(8)] +
                   [("wk", i) for i in range(2)] + [("wv", i) for i in range(4)])
        p2_list = ([("wq", i) for i in range(2)] + [("wg", i) for i in range(4)] +
                   [("wo", i) for i in range(4)] + [("up1", i) for i in range(8)] +
                   [("dn1", i) for i in range(8)])
        seq = []
        for _ in range(NT):
            seq += p1_list
        for _ in range(NT):
            seq += p2_list
        order = [SL[k] for k in p1_list] + [SL[k] for k in p2_list]
        order_pos = {sid: i for i, sid in enumerate(order)}
        cst8 = {"n": 0}
        CAST_AHEAD = 5

        def ensure_casts(n):
            while cst8["n"] < min(n, len(order)):
                sid = order[cst8["n"]]
                cst8["n"] += 1
                sl = slabs[sid]
                ch = chan(f"c_cast{sid}")
                if sl["kind"] == "win":
                    dst = wscr[sid][:, 0:3072].rearrange("p (kc g f) -> p kc g f", kc=8, g=3)
                    POOL.dma(ch, [dst[:, :, g, :] for g in range(3)], [sl["src"][:, :, g, :] for g in range(3)],
                             writes=[slab_R[sid]], n=3)
                else:
                    KC = sl["KC"]
                    dst = wscr[sid].rearrange("p (kc c) -> p kc c", kc=KC)
                    if KC == 32:
                        POOL.dma(ch, [dst[:, 0:16, :], dst[:, 16:32, :]],
                                 [sl["src"][:, 0:16, :], sl["src"][:, 16:32, :]],
                                 writes=[slab_R[sid]], n=2)
                    else:
                        POOL.dma(ch, dst, sl["src"], writes=[slab_R[sid]])

        wst = {"issued": 0, "next": 0}

        def w_issue_upto(n):
            while wst["issued"] < min(n, len(seq)):
                i = wst["issued"]
                sid = SL[seq[i]]
                ensure_casts(order_pos[sid] + 1 + CAST_AHEAD)
                s = i % NSLOT
                used = slabs[sid]["used"]
                SP.dma(slot_ch[s], wring[:, s, 0:used], wscr[sid][:, 0:used],
                       reads=[slab_R[sid]], writes=[slot_R[s]])
                wst["issued"] += 1

        def w_next(key):
            i = wst["next"]
            assert seq[i] == key, (seq[i], key)
            w_issue_upto(i + NSLOT)
            wst["next"] += 1
            s = i % NSLOT
            return wring[:, s, :], slot_R[s]

        c_cons = chan("c_cons")
        SP.dma(c_cons, [small[:], flagt[:], invt[:]], [small_d, flag_d, inv_d], writes=[R_const], n=3)

        def g_mix0(k): return small[:, 0 + k:1 + k]
        def g_mlp0(k): return small[:, 8 + k:9 + k]
        def g_mix1(k): return small[:, 16 + k:17 + k]
        def g_mlp1(k): return small[:, 24 + k:25 + k]
        def g_fin(k): return small[:, 32 + k:33 + k]
        def cw(j, k): return small[:, 40 + j * 8 + k:41 + j * 8 + k]
        def cb(k): return small[:, 64 + k:65 + k]
        flag_ap = flagt[:, 0:1]
        off_ap = flagt[:, 1:2]

        R_id = Region("ident")
        POOL.op(lambda: nc.gpsimd.iota(ident[:], [[1, 128]], base=0, channel_multiplier=-1,
                                       allow_small_or_imprecise_dtypes=True), writes=[R_id])
        POOL.op(lambda: nc.gpsimd.tensor_single_scalar(ident[:], ident[:], 0.0, ALU.is_equal),
                reads=[R_id], writes=[R_id])
        POOL.op(lambda: nc.gpsimd.tensor_copy(identb[:], ident[:]), reads=[R_id], writes=[R_const])
        POOL.op(lambda: nc.gpsimd.memset(onesb[:], 1.0), writes=[R_const])
        POOL.op(lambda: nc.gpsimd.memset(misc[:, 56:64], 0.0), writes=[R_tbl])
        POOL.op(lambda: nc.gpsimd.memset(S[:], 0.0), writes=S_R)
        POOL.op(lambda: nc.gpsimd.memset(Sbf[:], 0.0), writes=Sbf_R[0] + Sbf_R[1])

        tA = hm_f32(0); tB = hm_f32(1); tC = hm_f32(2); tD = hm_f32(3)
        RA = [HR[0], HR[1]]; RB = [HR[2], HR[3]]; RC = [HR[4], HR[5]]; RD = [HR[6], HR[7]]

        def gen_consts():
            dji = tA[:, 0:128]
            POOL.op(lambda: nc.gpsimd.iota(dji, [[1, 128]], base=0, channel_multiplier=-1,
                                           allow_small_or_imprecise_dtypes=True), writes=RA)
            idx = tA[:, 128:256]
            pidx = tA[:, 256:512]
            POOL.op(lambda: nc.gpsimd.iota(idx, [[1, 128]], base=0, channel_multiplier=0,
                                           allow_small_or_imprecise_dtypes=True), writes=RA)
            POOL.op(lambda: nc.gpsimd.iota(pidx, [[0, 256]], base=0, channel_multiplier=1,
                                           allow_small_or_imprecise_dtypes=True), writes=RA)
            a_ = tB[:, 0:128]; b_ = tB[:, 128:256]; mf = tB[:, 256:384]; mb = tB[:, 384:512]
            DVE.op(lambda: nc.vector.tensor_scalar_max(a_, dji, 0.0), reads=RA, writes=RB)
            DVE.op(lambda: nc.vector.tensor_scalar(b_, dji, -1.0, 0.0, ALU.mult, ALU.max), reads=RA, writes=RB)
            DVE.op(lambda: nc.vector.tensor_single_scalar(mf, dji, 0.0, ALU.is_ge), reads=RA, writes=RB)
            DVE.op(lambda: nc.vector.tensor_single_scalar(mb, dji, 0.0, ALU.is_lt), reads=RA, writes=RB)
            for h in range(4):
                ef = tC[:, 0:128]; eb = tC[:, 128:256]
                ACT.op(lambda: nc.scalar.activation(ef, a_, AF.Exp, scale=LGF[h]), reads=RB, writes=RC)
                ACT.op(lambda: nc.scalar.activation(eb, b_, AF.Exp, scale=LGB[h]), reads=RB, writes=RC)
                DVE.op(lambda: nc.vector.tensor_tensor(ef, ef, mf, ALU.mult), reads=RB + RC, writes=RC)
                DVE.op(lambda: nc.vector.tensor_tensor(eb, eb, mb, ALU.mult), reads=RB + RC, writes=RC)
                DVE.op(lambda: nc.vector.scalar_tensor_tensor(DT[:, h, :], ef, 1.0, eb, ALU.mult, ALU.add),
                       reads=RC, writes=[R_const])
                DVE.op(lambda: nc.vector.tensor_scalar_mul(DT[:, h, :], DT[:, h, :], QS),
                       reads=[R_const], writes=[R_const])
                ACT.op(lambda: nc.scalar.activation(decqf[:, h, :], idx, AF.Exp, scale=LGF[h],
                                                    bias=LGF[h] + math.log(QS)), reads=RA, writes=[R_const])
                ACT.op(lambda: nc.scalar.activation(decqb[:, h, :], idx, AF.Exp, scale=-LGB[h],
                                                    bias=128.0 * LGB[h] + math.log(QS)), reads=RA, writes=[R_const])
                ACT.op(lambda: nc.scalar.activation(kdf[:, h * 256:(h + 1) * 256], pidx, AF.Exp,
                                                    scale=-LGF[h], bias=127.0 * LGF[h]), reads=RA, writes=[R_kd])
                ACT.op(lambda: nc.scalar.activation(kdb[:, h * 256:(h + 1) * 256], pidx, AF.Exp,
                                                    scale=LGB[h]), reads=RA, writes=[R_kd])

        kint = sb("kint", [128, T], mybir.dt.int32)
        R_ki = Region("kint")
        cs_R = [Region(f"cs_scr{t}") for t in range(NT)]
        c_csst = chan("c_csst")
        def gen_cs_iota(t):
            POOL.op(lambda: nc.gpsimd.iota(tD, [[1, T]], base=t * T, channel_multiplier=0,
                                           allow_small_or_imprecise_dtypes=True), writes=RD)

        def gen_cs(t, do_iota=True):
            pos = tD
            if do_iota:
                gen_cs_iota(t)
            if t >= NT // 2:
                DVE.op(lambda: nc.vector.tensor_scalar(pos, pos, off_ap, None, ALU.subtract),
                       reads=RD + [R_const], writes=RD)
            DVE.op(lambda: nc.vector.tensor_scalar(pos, pos, invt[:, 0:1], None, ALU.mult),
                   reads=RD + [R_const], writes=RD)
            for half, shift in ((0, 0.25), (1, 0.0)):
                tt = tB; ki = kint[:]; kf = tA
                DVE.op(lambda: nc.vector.tensor_scalar(tt, pos, 1.0 / (2.0 * PI), shift, ALU.mult, ALU.add),
                       reads=RD, writes=RB)
                DVE.op(lambda: nc.vector.tensor_copy(ki, tt), reads=RB, writes=[R_ki])
                DVE.op(lambda: nc.vector.tensor_copy(kf, ki), reads=[R_ki], writes=RA)
                DVE.op(lambda: nc.vector.tensor_tensor(tt, tt, kf, ALU.subtract), reads=RA + RB, writes=RB)
                DVE.op(lambda: nc.vector.tensor_single_scalar(kf, tt, 0.5, ALU.is_gt), reads=RB, writes=RA)
                DVE.op(lambda: nc.vector.tensor_tensor(tt, tt, kf, ALU.subtract), reads=RA + RB, writes=RB)
                DVE.op(lambda: nc.vector.tensor_scalar(cst[:, half, :], tt, 2.0 * PI_LO, None, ALU.mult),
                       reads=RB, writes=[cst_R])
            ACT.op(lambda: nc.scalar.activation(cst[:], cst[:], AF.Sin), reads=[cst_R], writes=[cst_R])
            act_preload(AF.Ln)
            SP.dma(c_csst, cs_scr[t], cst[:], reads=[cst_R], writes=[cs_R[t]])

        def rms_sq(xs, xs_R, sq, sq_R):
            for k in range(8):
                ACT.op(lambda: nc.scalar.activation(sq[k], xs[k], AF.Square),
                       reads=[xs_R[k]], writes=[sq_R[k]])

        def rmsnorm(N, xs, xs_R, gfn, outs, outs_R, rs, rs_R, sq=None, sq_R=None, do_sq=True):
            if sq is None:
                sq, sq_R = outs, outs_R
            bk, bR = bank("side")
            if do_sq:
                rms_sq(xs, xs_R, sq, sq_R)
            for k in range(8):
                PE.op(lambda: nc.tensor.matmul(bk[:, 0:N], onesb[:], sq[k], start=(k == 0), stop=(k == 7)),
                      reads=[sq_R[k], R_const], writes=[bR], signal=(k == 7))
            ACT.op(lambda: nc.scalar.activation(rs, bk[:, 0:N], AF.Ln, bias=EPS, scale=1.0 / D),
                   reads=[bR], writes=[rs_R])
            ACT.op(lambda: nc.scalar.activation(rs, rs, AF.Exp, scale=-0.5), reads=[rs_R], writes=[rs_R])
            for k in range(8):
                DVE.op(lambda: nc.vector.scalar_tensor_tensor(outs[k], xs[k], gfn(k), rs, ALU.mult, ALU.mult),
                       reads=[xs_R[k], rs_R, R_const], writes=[outs_R[k]])

        WARM = 24

        def keep_warm(bk, bR, n=None):
            prev = STAGE[0]
            STAGE[0] = "warm"
            for _ in range(WARM if n is None else n):
                PE.op(lambda: nc.tensor.matmul(bk[:, 0:128], onesb[:], onesb[:], start=True, stop=True),
                      reads=[R_const], writes=[bR], signal=False)
            STAGE[0] = prev

        def proj_fm(slab, sR, KC, ncols_chunks, rhs, rhs_R, evac, warm=False):
            C = 4096 // KC
            if warm and ncols_chunks > 1:
                bks = [bank() for _ in range(ncols_chunks)]
                keep_warm(*bks[0])
                for k in range(KC):
                    for mm in range(ncols_chunks):
                        bk, bR = bks[mm]
                        PE.op(lambda: nc.tensor.matmul(bk[:], slab[:, k * C + mm * 128:k * C + (mm + 1) * 128],
                                                       rhs[k], start=(k == 0), stop=(k == KC - 1)),
                              reads=[sR, rhs_R[k]], writes=[bR], signal=(k == KC - 1))
                for mm in range(ncols_chunks):
                    evac(mm, *bks[mm])
                return
            for mm in range(ncols_chunks):
                bk, bR = bank()
                if warm and mm == 0:
                    keep_warm(bk, bR)
                for k in range(KC):
                    PE.op(lambda: nc.tensor.matmul(bk[:], slab[:, k * C + mm * 128:k * C + (mm + 1) * 128],
                                                   rhs[k], start=(k == 0), stop=(k == KC - 1)),
                          reads=[sR, rhs_R[k]], writes=[bR], signal=(k == KC - 1))
                evac(mm, bk, bR)

        xT_k = [xT[:, k, :] for k in range(8)]
        xn_k = [xn[:, k, :] for k in range(8)]
        hm_k = [hm[:, j, :] for j in range(32)]

        def mlp(up_name, dn_name, gfn, XK=None, XR=None, mid_hook=None, slab_hook=None):
            if XK is None:
                XK, XR = xT_k, xT_R
            base = STAGE[0]
            STAGE[0] = base + ".N2"
            rmsnorm(T, XK, XR, gfn, xn_k, xn_R, rstd[:, 0, :], rstd_R[0])
            STAGE[0] = base + ".up"
            for i in range(8):
                slab, sR = w_next((up_name, i))

                def evac(mm, bk, bR, i=i):
                    j = i * 4 + mm
                    rb = 1 + (j % 2)
                    rt = rstd[:, rb, :]
                    ACT.op(lambda: nc.scalar.activation(rt, bk[:], AF.Relu), reads=[bR], writes=[rstd_R[rb]])
                    POOL.op(lambda: nc.gpsimd.tensor_tensor(hm_k[j], rt, rt, ALU.mult),
                            reads=[rstd_R[rb]], writes=[HR[j]])
                proj_fm(slab, sR, 8, 4, xn_k, xn_R, evac, warm=(i == 0))
                if slab_hook is not None:
                    slab_hook(i)
            if mid_hook is not None:
                mid_hook()
            STAGE[0] = base + ".dn"
            for i in range(8):
                slab, sR = w_next((dn_name, i))

                def evac(mm, bk, bR, i=i):
                    DVE.op(lambda: nc.vector.tensor_tensor(XK[i], bk[:], XK[i], ALU.add),
                           reads=[bR, XR[i]], writes=[XR[i]])
                proj_fm(slab, sR, 32, 1, hm_k, HR, evac)

        rot = sb("rot", [128, 4, T], F32)
        rot_R = [Region(f"rot{i}") for i in range(4)]
        xe = rot[0:16, 0:2, :].rearrange("p a b -> p (a b)")
        xeT = sb("xeT", [128, 8, 16], F32)
        xne = sb("xne", [128, 8, 16], BF16)
        uedge = sb("uedge", [128, 8, 16], F32)
        R_xeT = [Region(f"xeT{k}") for k in range(8)]
        R_xne = [Region(f"xne{k}") for k in range(8)]; R_ue = Region("uedge")
        c_xe = chan("c_xe")
        R_xe = rot_R[0]
        POOL.op(lambda: nc.gpsimd.memset(xe, 0.0), writes=[rot_R[0], rot_R[1]])
        SP.dma(c_xe, [xe[2 * (b - 1):2 * b, :] for b in range(1, 8)],
               [x_in[T * b - 1:T * b + 1, :] for b in range(1, 8)], writes=[R_xe], n=7)
        STAGE[0] = "edge"
        bk, bR = bank()
        for k in range(8):
            PE.op(lambda: nc.tensor.transpose(bk[:, k * 16:(k + 1) * 16], xe[0:16, k * 128:(k + 1) * 128],
                                              ident[0:16, 0:16]),
                  reads=[R_xe, R_id], writes=[bR], signal=(k == 7))
        DVE.op(lambda: nc.vector.tensor_copy(xeT[:].rearrange("p k n -> p (k n)"), bk[:, 0:128]),
               reads=[bR], writes=R_xeT)
        rmsnorm(16, [xeT[:, k, :] for k in range(8)], R_xeT, g_mix0,
                [xne[:, k, :] for k in range(8)], R_xne, rstd[:, 0, 0:16], rstd_R[0])
        R_uec = [Region(f"uedge{c}") for c in range(8)]

        def edge_chunk(c, slab, sR):
            prev = STAGE[0]
            STAGE[0] = "edge"
            bh, bhR = bank()
            bg, bgR = bank()
            for j, (bk_, bR_) in ((0, (bh, bhR)), (2, (bg, bgR))):
                for k in range(8):
                    PE.op(lambda: nc.tensor.matmul(bk_[:, 0:16], slab[:, k * 384 + j * 128:k * 384 + (j + 1) * 128],
                                                   xne[:, k, :], start=(k == 0), stop=(k == 7)),
                          reads=[sR, R_xne[k]], writes=[bR_], signal=(k == 7))
            he = misc[:, 0:16]
            ACT.op(lambda: nc.scalar.copy(he, bh[:, 0:16]), reads=[bhR], writes=[misc_R[0]])
            DVE.op(lambda: nc.vector.tensor_tensor(uedge[:, c, :], bg[:, 0:16], he, ALU.mult),
                   reads=[bgR, misc_R[0]], writes=[R_uec[c]])
            DVE.op(lambda: nc.vector.tensor_scalar(uedge[:, c, 6:8], uedge[:, c, 6:8], flag_ap, None, ALU.mult),
                   reads=[R_uec[c], R_const], writes=[R_uec[c]])
            STAGE[0] = prev

        xin = sb("xin", [128, 4, D], F32)
        xin_R = [Region(f"xin{c}") for c in range(4)]
        c_xin = [chan(f"c_xin{c}") for c in range(4)]
        kvA = sb("kvA", [128, 24, T], BF16)
        ktm = sb("ktm", [128, 1024], BF16)
        ktm_R = Region("ktm")
        vtm = sb("vtm", [128, 4, 2048], BF16)
        vtm_R = [[Region(f"v{c}_{h}") for h in range(4)] for c in range(4)]
        kfm = [kvA[:, m, :] for m in range(8)]
        kfm_R = [Region(f"kfm{m}") for m in range(8)]
        kft = [kvA[:, 8 + 2 * c:10 + 2 * c, :].rearrange("p a b -> p (a b)") for c in range(4)]
        kft_R = [Region(f"kft{c}") for c in range(4)]
        kbt = [kvA[:, 16 + 2 * c:18 + 2 * c, :].rearrange("p a b -> p (a b)") for c in range(4)]
        kbt_R = [Region(f"kbt{c}") for c in range(4)]
        c_x1st = chan("c_x1st"); c_kst = chan("c_kst"); c_kfst = chan("c_kfst"); c_vst = chan("c_vst")
        x1_R = [Region(f"x1_scr{t}") for t in range(NT)]
        ks_R = [Region(f"k_scr{t}") for t in range(NT)]
        kfs_R = [Region(f"kf_scr{t}") for t in range(NT)]
        vs_R = [Region(f"v_scr{t}") for t in range(NT)]
        sbs_R = [Region(f"sb_scr{n}") for n in range(NCH)]

        u_k = [hm_f32(c) for c in range(8)]
        u_R = [[HR[2 * c], HR[2 * c + 1]] for c in range(8)]
        gz_k = [hm[:, 16 + c, :] for c in range(8)]
        gz_R = [HR[16 + c] for c in range(8)]
        htmp = [hm_f32(12), hm_f32(13)]
        htmp_R = [[HR[24], HR[25]], [HR[26], HR[27]]]
        ztmp = [hm_f32(14), hm_f32(15)]
        ztmp_R = [[HR[28], HR[29]], [HR[30], HR[31]]]

        def load_cs(t):
            SP.dma(c_cst, cst[:], cs_scr[t], reads=[cs_R[t]], writes=[cst_R])

        def rotary(b1, b1R, b2, b2R, emit1, emit2, comb="pool"):
            cos = cst[:, 0, :]; sin = cst[:, 1, :]
            DVE.op(lambda: nc.vector.tensor_tensor(rot[:, 0, :], b1[:], cos, ALU.mult),
                   reads=[b1R, cst_R], writes=[rot_R[0]])
            DVE.op(lambda: nc.vector.tensor_tensor(rot[:, 1, :], b2[:], sin, ALU.mult),
                   reads=[b2R, cst_R], writes=[rot_R[1]])
            ce, ch_ = (POOL, nc.gpsimd) if comb == "pool" else (DVE, nc.vector)
            ce.op(lambda: ch_.tensor_tensor(rot[:, 0, :], rot[:, 0, :], rot[:, 1, :], ALU.subtract),
                  reads=[rot_R[0], rot_R[1]], writes=[rot_R[0]])
            emit1(rot[:, 0, :], rot_R[0])
            DVE.op(lambda: nc.vector.tensor_tensor(rot[:, 2, :], b1[:], sin, ALU.mult),
                   reads=[b1R, cst_R], writes=[rot_R[2]])
            DVE.op(lambda: nc.vector.tensor_tensor(rot[:, 3, :], b2[:], cos, ALU.mult),
                   reads=[b2R, cst_R], writes=[rot_R[3]])
            ce.op(lambda: ch_.tensor_tensor(rot[:, 2, :], rot[:, 2, :], rot[:, 3, :], ALU.add),
                  reads=[rot_R[2], rot_R[3]], writes=[rot_R[2]])
            emit2(rot[:, 2, :], rot_R[2])

        def state_update(n, kt, ktR, vt_c, vt_cR, gC, sbuf_i, boundary, grp="main", heads=(0, 1, 2, 3)):
            for h in heads:
                for dc in range(2):
                    i = h * 2 + dc
                    bk, bR = bank(grp)
                    PE.op(lambda: nc.tensor.matmul(bk[:], kt[:, h * 256 + dc * 128:h * 256 + (dc + 1) * 128],
                                                   vt_c[:, h * 512:(h + 1) * 512], start=True, stop=True),
                          reads=[ktR] + (vt_cR[h] if isinstance(vt_cR[h], list) else [vt_cR[h]]), writes=[bR])
                    DVE.op(lambda: nc.vector.scalar_tensor_tensor(S[:, i, :], S[:, i, :], gC[h], bk[:],
                                                                  ALU.mult, ALU.add),
                           reads=[bR, S_R[i]], writes=[S_R[i]])
                    if boundary:
                        DVE.op(lambda: nc.vector.tensor_scalar(S[:, i, :], S[:, i, :], flag_ap, None, ALU.mult),
                               reads=[S_R[i], R_const], writes=[S_R[i]])
                    ACT.op(lambda: nc.scalar.copy(Sbf[:, sbuf_i, i, :], S[:, i, :]),
                           reads=[S_R[i]], writes=[Sbf_R[sbuf_i][i]])

        GCF = [math.exp(128.0 * LGF[h]) for h in range(4)]
        GCB = [math.exp(128.0 * LGB[h]) for h in range(4)]

        sbf = {"cur": 0}

        def kT_chunk(t, c):
            prev = STAGE[0]
            STAGE[0] = "P1.S7kT"
            bk, bR = bank("side")
            bkb = bk[:].bitcast(BF16)
            for m in range(8):
                PE.op(lambda: nc.tensor.transpose(bkb[:, m * 128:(m + 1) * 128],
                                                  kfm[m][:, c * 128:(c + 1) * 128], identb[:]),
                      reads=[kfm_R[m], R_const], writes=[bR], signal=(m == 7))
            for h in range(4):
                ACT.op(lambda: nc.scalar.activation(kbt[c][:, h * 256:(h + 1) * 256], bkb[:, h * 256:(h + 1) * 256],
                                                    AF.Identity, scale=kdb[:, h * 256:h * 256 + 1]),
                       reads=[bR, R_kd], writes=[kbt_R[c]])
            for h in range(4):
                ACT.op(lambda: nc.scalar.activation(kft[c][:, h * 256:(h + 1) * 256], bkb[:, h * 256:(h + 1) * 256],
                                                    AF.Identity, scale=kdf[:, h * 256:h * 256 + 1]),
                       reads=[bR, R_kd], writes=[kft_R[c]])
            STAGE[0] = prev

        def kT_tail(t):
            kT_chunk(t, 2)
            kT_chunk(t, 3)
            ACT.dma(c_kfst, kf_scr[4 * t:4 * t + 4].rearrange("c p f -> p c f"),
                   kvA[:, 8:16, :].rearrange("p (c a) b -> p c (a b)", c=4), reads=kft_R, writes=[kfs_R[t]])

        def s9_chunk(t, c):
            prev = STAGE[0]
            STAGE[0] = "P1.S9state"
            n = 4 * t + c
            cur = sbf["cur"]
            ACT.dma(c_sbf[cur], sb_scr[n], Sbf[:, cur], reads=Sbf_R[cur], writes=[sbs_R[n]])
            sbf["cur"] = cur ^ 1
            state_update(n, kbt[c], kbt_R[c], vtm[:, c, :], vtm_R[c], GCB, cur ^ 1, boundary=(n == 16))
            STAGE[0] = prev

        def s9(t):
            for c in range(3, -1, -1):
                s9_chunk(t, c)

        def issue_xin(t):
            for c in range(4):
                POOL.dma(c_xin[c], xin[:, c, :], x_in[t * T + c * 128:t * T + (c + 1) * 128, :], writes=[xin_R[c]])
        issue_xin(NT - 1)
        gen_consts()
        gen_cs(NT - 1)
        for t in range(NT - 1, -1, -1):
            STAGE[0] = "P1.S1in"
            for k in range(8):
                bk, bR = bank()
                for c in range(4):
                    PE.op(lambda: nc.tensor.transpose(bk[:, c * 128:(c + 1) * 128],
                                                      xin[:, c, k * 128:(k + 1) * 128], ident[:]),
                          reads=[xin_R[c], R_id], writes=[bR], signal=(c == 3))
                if k % 2 == 0:
                    ACT.op(lambda: nc.scalar.copy(xT_k[k], bk[:]), reads=[bR], writes=[xT_R[k]])
                else:
                    DVE.op(lambda: nc.vector.tensor_copy(xT_k[k], bk[:]), reads=[bR], writes=[xT_R[k]])
            if t > 0:
                issue_xin(t - 1)
            STAGE[0] = "P1.N1"
            rmsnorm(T, xT_k, xT_R, g_mix0, xn_k, xn_R, rstd[:, 0, :], rstd_R[0])
            if t < NT - 1:
                kT_tail(t + 1)
            STAGE[0] = "P1.S3win"
            for c in range(8):
                slab, sR = w_next(("win", c))
                if t == NT - 1:
                    edge_chunk(c, slab, sR)
                bs = [bank() for _ in range(3)]
                if c == 0:
                    keep_warm(*bs[0])
                    for k in range(8):
                        for j in (0, 2, 1):
                            bk_, bR_ = bs[j]
                            PE.op(lambda: nc.tensor.matmul(bk_[:], slab[:, k * 384 + j * 128:k * 384 + (j + 1) * 128],
                                                           xn_k[k], start=(k == 0), stop=(k == 7)),
                                  reads=[sR, xn_R[k]], writes=[bR_], signal=(k == 7))
                for j in ((0, 2, 1) if c > 0 else ()):
                    bk_, bR_ = bs[j]
                    for k in range(8):
                        PE.op(lambda: nc.tensor.matmul(bk_[:], slab[:, k * 384 + j * 128:k * 384 + (j + 1) * 128],
                                                       xn_k[k], start=(k == 0), stop=(k == 7)),
                              reads=[sR, xn_R[k]], writes=[bR_], signal=(k == 7))
                (bh, bhR), (bgb, bgbR), (bgc, bgcR) = bs
                hb = c % 2
                ACT.op(lambda: nc.scalar.copy(htmp[hb], bh[:]), reads=[bhR], writes=htmp_R[hb])
                DVE.op(lambda: nc.vector.tensor_tensor(u_k[c], bgc[:], htmp[hb], ALU.mult),
                       reads=[bgcR] + htmp_R[hb], writes=u_R[c])
                z = ztmp[hb]; zR = ztmp_R[hb]
                ACT.op(lambda: nc.scalar.activation(z, u_k[c], AF.Identity, bias=cb(c), scale=cw(1, c)),
                       reads=u_R[c] + [R_const], writes=zR)
                DVE.op(lambda: nc.vector.scalar_tensor_tensor(z[:, 1:T], u_k[c][:, 0:T - 1], cw(0, c), z[:, 1:T],
                                                               ALU.mult, ALU.add),
                        reads=u_R[c] + zR + [R_const], writes=zR)
                DVE.op(lambda: nc.vector.scalar_tensor_tensor(z[:, 0:T - 1], u_k[c][:, 1:T], cw(2, c), z[:, 0:T - 1],
                                                               ALU.mult, ALU.add),
                        reads=u_R[c] + zR + [R_const], writes=zR)
                if t >= 1:
                    e = 2 * (t - 1)
                    DVE.op(lambda: nc.vector.scalar_tensor_tensor(z[:, 0:1], uedge[:, c, e:e + 1], cw(0, c), z[:, 0:1],
                                                                   ALU.mult, ALU.add),
                            reads=[R_uec[c], R_const] + zR, writes=zR)
                if t <= NT - 2:
                    e = 2 * t + 1
                    DVE.op(lambda: nc.vector.scalar_tensor_tensor(z[:, T - 1:T], uedge[:, c, e:e + 1], cw(2, c),
                                                                   z[:, T - 1:T], ALU.mult, ALU.add),
                            reads=[R_uec[c], R_const] + zR, writes=zR)
                DVE.op(lambda: nc.vector.tensor_tensor(gz_k[c], bgb[:], z, ALU.mult),
                       reads=[bgbR] + zR, writes=[gz_R[c]])
            STAGE[0] = "P1.S4wout"
            for i in range(2):
                slab, sR = w_next(("wout", i))

                def evac(mm, bk, bR, i=i):
                    m = i * 4 + mm
                    DVE.op(lambda: nc.vector.tensor_tensor(xT_k[m], bk[:], xT_k[m], ALU.add),
                           reads=[bR, xT_R[m]], writes=[xT_R[m]])
                proj_fm(slab, sR, 8, 4, gz_k, gz_R, evac)
            STAGE[0] = "P1.mlp"
            mlp("up0", "dn0", g_mlp0,
                slab_hook=(lambda i, t=t: s9_chunk(t + 1, 3 - i // 2) if i % 2 == 0 else None) if t < NT - 1 else None)
            STAGE[0] = "P1.N3"
            if t > 0:
                gen_cs_iota(t - 1)
            POOL.dma(c_x1st, x1_scr[t], xT[:], reads=xT_R, writes=[x1_R[t]])
            rmsnorm(T, xT_k, xT_R, g_mix1, xn_k, xn_R, rstd[:, 0, :], rstd_R[0])
            STAGE[0] = "P1.S7k"
            for i in range(2):
                slab, sR = w_next(("wk", i))
                pend = []

                def evac(mm, bk, bR, i=i, pend=pend):
                    pend.append((bk, bR))
                    if len(pend) == 2:
                        (b1, b1R), (b2, b2R) = pend
                        m0 = i * 4 + (mm - 1)

                        def e1(r, rR, m0=m0):
                            ACT.op(lambda: nc.scalar.copy(kfm[m0], r), reads=[rR], writes=[kfm_R[m0]])

                        def e2(r, rR, m0=m0):
                            ACT.op(lambda: nc.scalar.copy(kfm[m0 + 1], r), reads=[rR], writes=[kfm_R[m0 + 1]])
                        rotary(b1, b1R, b2, b2R, e1, e2)
                        pend.clear()
                proj_fm(slab, sR, 8, 4, xn_k, xn_R, evac, warm=(i == 0))
            ACT.dma(c_kst, k_scr[t], kvA[:, 0:8, :], reads=kfm_R, writes=[ks_R[t]])
            STAGE[0] = "P1.S8v"
            for h in range(4):
                slab, sR = w_next(("wv", h))
                if h in (1, 2):
                    kT_chunk(t, h - 1)
                if h == 2 and t > 0:
                    gen_cs(t - 1, do_iota=False)
                for c in range(4):
                    bk, bR = bank()
                    for k in range(8):
                        PE.op(lambda: nc.tensor.matmul(bk[:], xn_k[k][:, c * 128:(c + 1) * 128],
                                                       slab[:, k * 512:(k + 1) * 512], start=(k == 0), stop=(k == 7)),
                              reads=[sR, xn_R[k]], writes=[bR], signal=(k == 7))
                    ACT.op(lambda: nc.scalar.copy(vtm[:, c, h * 512:(h + 1) * 512], bk[:]),
                           reads=[bR], writes=[vtm_R[c][h]])
            ACT.dma(c_vst, v_scr[4 * t:4 * t + 4].rearrange("c p f -> p c f"), vtm[:],
                   reads=[r for c in range(4) for r in vtm_R[c]], writes=[vs_R[t]])
            STAGE[0] = "P1.S9state"

        kT_tail(0)
        s9(0)

        q_k = [hm[:, m, :] for m in range(8)]; q_R = [HR[m] for m in range(8)]
        qf_k = [hm[:, 8 + m, :] for m in range(8)]; qf_R = [HR[8 + m] for m in range(8)]
        qb_k = [hm[:, 16 + m, :] for m in range(8)]; qb_R = [HR[16 + m] for m in range(8)]
        k2 = [hm[:, 24 + m, :] for m in range(8)]; k2_R = [HR[24 + m] for m in range(8)]
        c_k2 = chan("c_k2")
        gact = vtm
        gact_R = vtm_R
        gT = xin[:].rearrange("p c f -> p (c f)").bitcast(BF16).rearrange("p (j n) -> p j n", j=16)
        gT_R = [Region(f"gT{j}") for j in range(16)]
        kf2s = [ktm[:], kvA[:, 22:24, :].rearrange("p a b -> p (a b)")]
        kf2s_R = [ktm_R, Region("kf2b")]
        c_kf2 = [chan("c_kf2a"), chan("c_kf2b")]

        def ret_loads_kv(n):
            b = n % 2
            SP.dma(c_kf2[b], kf2s[b], kf_scr[n], reads=[kfs_R[n // 4]], writes=[kf2s_R[b]])
            SP.dma(c_v2[b], v2s[b], v_scr[n], reads=[vs_R[n // 4]], writes=[r for h in range(4) for r in v2s_R[b][h]])

        def ret_loads_sb(n):
            for h in range(4):
                SP.dma(c_sbl[h], sbl[:, 2 * h:2 * h + 2, :], sb_scr[n][:, 2 * h:2 * h + 2, :],
                       reads=[sbs_R[n]], writes=[sbl_R[h]])
        v2s = [kdf[:].bitcast(BF16),
               rot[:, 2:4, :].rearrange("p a b -> p (a b)").bitcast(BF16)]
        v2s_R = [[[Region(f"v2_{h}")] for h in range(4)], [[rot_R[2], rot_R[3]] for h in range(4)]]
        c_v2 = [chan("c_v2a"), chan("c_v2b")]
        sbl = kvA[:, 0:8, :]
        sbl_R = [Region(f"sbl{h}") for h in range(4)]
        c_sbl = [chan(f"c_sbl{h}") for h in range(4)]
        PT = kvA[:, 8:10, :]
        PT_R = [Region("PT0"), Region("PT1")]
        gated = kvA[:, 10:18, :]
        gated_R = [[Region(f"gated{b}_{h}") for h in range(4)] for b in range(2)]
        on_ = [rot[:, 0, :], rot[:, 1, :]]
        youts = [rstd[:, 1:3, :].rearrange("p a b -> p (a b)"),
                 rot[:, 0:2, :].rearrange("p a b -> p (a b)")]
        youts_R = [[rstd_R[1], rstd_R[2]], [rot_R[0], rot_R[1]]]
        c_yout = [chan("c_yout0"), chan("c_yout1")]
        c_x1ld = [chan(f"c_x1ld{k}") for k in range(8)]

        def p2_prefetch_k(t):
            POOL.dma(c_k2, hm[:, 24:32, :], k_scr[t], reads=[ks_R[t]], writes=k2_R)
        out_toks = []

        engs = [PE, ACT, DVE, POOL]
        Rbar = Region("bar")
        STAGE[0] = "barrier"
        for e in engs:
            if e is PE:
                e.op(lambda: nc.tensor.matmul(banks[0][:, 0:16], onesb[:], onesb[:, 0:16], start=True, stop=True),
                     reads=[R_const], writes=[bank_R[0]])
        allR = (kfm_R + kft_R + kbt_R + [ktm_R] + [r for c in range(4) for r in vtm_R[c]] + rot_R + HR +
                Sbf_R[0] + Sbf_R[1] + S_R + xin_R + [R_kd])
        for e in (ACT, DVE, POOL, SP, PE):
            e._deps([], allR + bank_R)
        for R in allR:
            R.w = None
            R.r = {}
        POOL.op(lambda: nc.gpsimd.memset(S[:], 0.0), writes=S_R)
        POOL.op(lambda: nc.gpsimd.memset(Sbf[:, 0], 0.0), writes=Sbf_R[0])
        SfB = Sbf_R[0]

        def f32v(ap3):
            return ap3.rearrange("p a b -> p (a b)").bitcast(F32)
        xTb_k = ([f32v(Sbf[:, 1, 2 * j:2 * j + 2, :]) for j in range(4)] +
                 [f32v(kvA[:, 18 + 2 * j:20 + 2 * j, :]) for j in range(2)] +
                 [kdb[:, 0:512], kdb[:, 512:1024]])
        xTb_R = [Region(f"xTb{k}") for k in range(8)]
        XT = [(xT_k, xT_R), (xTb_k, xTb_R)]

        def p2_prefetch(t):
            XK, XR = XT[t % 2]
            for k in range(8):
                POOL.dma(c_x1ld[k], XK[k], x1_scr[t][:, k, :], reads=[x1_R[t]], writes=[XR[k]])

        rotb = rot[:].rearrange("p a b -> p (a b)").bitcast(BF16)
        sqn_k = [rotb[:, j * T:(j + 1) * T] for j in range(8)]
        sqn_R = [rot_R[j // 2] for j in range(8)]

        def p2_n1a(t):
            XK, XR = XT[t % 2]
            rms_sq(XK, XR, sqn_k, sqn_R)

        def p2_n1b(t):
            prev = STAGE[0]
            STAGE[0] = "P2.N1"
            XK, XR = XT[t % 2]
            rmsnorm(T, XK, XR, g_mix1, xn_k, xn_R, rstd[:, 0, :], rstd_R[0], sq=sqn_k, sq_R=sqn_R, do_sq=False)
            STAGE[0] = prev

        def p2_n1(t):
            p2_n1a(t)
            p2_n1b(t)

        def p2_q(t, slabs=(0, 1), deferred=None):
            prev = STAGE[0]
            STAGE[0] = "P2.q"
            for i in slabs:
                slab, sR = w_next(("wq", i))
                pend = []

                def evac(mm, bk, bR, i=i, pend=pend):
                    pend.append((bk, bR))
                    if len(pend) == 2:
                        (b1, b1R), (b2, b2R) = pend
                        m0 = i * 4 + (mm - 1)
                        h = m0 // 2

                        def mk(m):
                            def e(r, rR):
                                ACT.op(lambda: nc.scalar.copy(q_k[m], r), reads=[rR], writes=[q_R[m]])
                                r3 = r.rearrange("p (c i) -> p c i", c=4)
                                POOL.op(lambda: nc.gpsimd.tensor_tensor(
                                    qf_k[m].rearrange("p (c i) -> p c i", c=4), r3,
                                    decqf[:, h:h + 1, :].to_broadcast([128, 4, 128]), ALU.mult),
                                    reads=[rR, R_const], writes=[qf_R[m]])
                                POOL.op(lambda: nc.gpsimd.tensor_tensor(
                                    qb_k[m].rearrange("p (c i) -> p c i", c=4), r3,
                                    decqb[:, h:h + 1, :].to_broadcast([128, 4, 128]), ALU.mult),
                                    reads=[rR, R_const], writes=[qb_R[m]])
                            return e

                        def run(b1=b1, b1R=b1R, b2=b2, b2R=b2R, m0=m0):
                            rotary(b1, b1R, b2, b2R, mk(m0), mk(m0 + 1), comb="dve")
                        if deferred is None:
                            run()
                        else:
                            deferred.append(run)
                        pend.clear()
                proj_fm(slab, sR, 8, 4, xn_k, xn_R, evac)
            STAGE[0] = prev

        p2_prefetch(0)
        load_cs(0)
        p2_prefetch_k(0)
        ret_loads_kv(0)
        ret_loads_sb(0)
        p2_n1(0)
        p2_q(0)
        for t in range(NT):
            XK, XR = XT[t % 2]
            STAGE[0] = "P2.g"
            for gi in range(4):
                slab, sR = w_next(("wg", gi))
                for c in range(4):
                    bk, bR = bank()
                    for k in range(8):
                        PE.op(lambda: nc.tensor.matmul(bk[:], xn_k[k][:, c * 128:(c + 1) * 128],
                                                       slab[:, k * 512:(k + 1) * 512], start=(k == 0), stop=(k == 7)),
                              reads=[sR, xn_R[k]], writes=[bR], signal=(k == 7))
                    ACT.op(lambda: nc.scalar.activation(gact[:, c, gi * 512:(gi + 1) * 512], bk[:], AF.Silu),
                           reads=[bR], writes=[gact_R[c][gi]])
            act_preload(AF.Ln)
            if t + 1 < NT:
                p2_prefetch(t + 1)
                load_cs(t + 1)
            STAGE[0] = "P2.ret"
            for c in range(4):
                n = 4 * t + c
                cs_ = slice(c * 128, (c + 1) * 128)
                vb = n % 2
                v2 = v2s[vb]; v2_R = v2s_R[vb]; kf2 = kf2s[vb]; kf2_R = kf2s_R[vb]
                if n + 1 < NCH:
                    ret_loads_kv(n + 1)
                def emit_scores(cc):
                    prev = STAGE[0]
                    STAGE[0] = "P2.ret.sc"
                    csl = slice(cc * 128, (cc + 1) * 128)
                    bs_, bsR = bank("r")
                    for hh in range(4):
                        for dc in range(2):
                            m = 2 * hh + dc
                            PE.op(lambda: nc.tensor.matmul(bs_[:, hh * 128:(hh + 1) * 128], k2[m][:, csl], q_k[m][:, csl],
                                                           start=(dc == 0), stop=(dc == 1)),
                                  reads=[k2_R[m], q_R[m]], writes=[bsR], signal=(hh == 3 and dc == 1))
                    DVE.op(lambda: nc.vector.tensor_tensor(PT[:, cc % 2, :], bs_[:], DT[:].rearrange("p h i -> p (h i)"),
                                                           ALU.mult),
                           reads=[bsR, R_const], writes=[PT_R[cc % 2]])
                    STAGE[0] = prev

                if c == 0:
                    emit_scores(0)
                pb = c % 2
                PTb = PT[:, pb, :]
                gb_ = c % 2
                STAGE[0] = "P2.ret.o"
                obanks = []

                def gn_finish(h, c=c, gb_=gb_, obanks=obanks):
                    bo, boR = obanks[h]
                    ag = misc[:, 32 + 4 * h:32 + 4 * h + 2]
                    rs_ = misc[:, 32 + 4 * h + 2:32 + 4 * h + 3]; nm = misc[:, 32 + 4 * h + 3:32 + 4 * h + 4]
                    mR = misc_R[h]
                    DVE.op(lambda: nc.vector.scalar_tensor_tensor(nm, ag[:, 0:1], -1.0, rs_, ALU.mult, ALU.mult),
                           reads=[mR], writes=[mR])
                    ob = h % 2
                    ACT.op(lambda: nc.scalar.activation(on_[ob], bo[:], AF.Identity, bias=nm, scale=rs_),
                           reads=[boR, mR], writes=[rot_R[ob]])
                    POOL.op(lambda: nc.gpsimd.tensor_tensor(gated[:, 4 * gb_ + h, :], on_[ob],
                                                            gact[:, c, h * 512:(h + 1) * 512], ALU.mult),
                            reads=[rot_R[ob], gact_R[c][h]], writes=[gated_R[gb_][h]])

                for h in range(4):
                    bo, boR = bank("o")
                    PE.op(lambda: nc.tensor.matmul(bo[:], PTb[:, h * 128:(h + 1) * 128], v2[:, h * 512:(h + 1) * 512],
                                                   start=True, stop=False),
                          reads=[PT_R[pb]] + v2_R[h], writes=[boR], signal=False)
                    for dc in range(2):
                        m = 2 * h + dc
                        PE.op(lambda: nc.tensor.matmul(bo[:], qf_k[m][:, cs_], Sbf[:, 0, m, :], start=False, stop=False),
                              reads=[qf_R[m], SfB[m]], writes=[boR], signal=False)
                    for dc in range(2):
                        m = 2 * h + dc
                        PE.op(lambda: nc.tensor.matmul(bo[:], qb_k[m][:, cs_], sbl[:, m, :], start=False, stop=(dc == 1)),
                              reads=[qb_R[m], sbl_R[h]], writes=[boR], signal=(dc == 1))
                    st = misc[:, 8 * h:8 * h + 6]; ag = misc[:, 32 + 4 * h:32 + 4 * h + 2]
                    rs_ = misc[:, 32 + 4 * h + 2:32 + 4 * h + 3]
                    mR = misc_R[h]
                    DVE.op(lambda: nc.vector.bn_stats(st, bo[:]), reads=[boR], writes=[mR])
                    DVE.op(lambda: nc.vector.bn_aggr(ag, st), reads=[mR], writes=[mR])
                    ACT.op(lambda: nc.scalar.activation(rs_, ag[:, 1:2], AF.Ln, bias=EPS, scale=1.0), reads=[mR], writes=[mR])
                    ACT.op(lambda: nc.scalar.activation(rs_, rs_, AF.Exp, scale=-0.5), reads=[mR], writes=[mR])
                    obanks.append((bo, boR))
                    if h >= 1:
                        gn_finish(h - 1)
                    if h >= 2:
                        STAGE[0] = "P2.ret.st"
                        state_update(n, kf2, kf2_R, v2, v2_R, GCF, 0, boundary=(n == 15), grp="r", heads=(h - 2,))
                        STAGE[0] = "P2.ret.o"
                gn_finish(3)
                if c < 3:
                    emit_scores(c + 1)
                if n + 1 < NCH:
                    ret_loads_sb(n + 1)
                STAGE[0] = "P2.ret.st"
                state_update(n, kf2, kf2_R, v2, v2_R, GCF, 0, boundary=(n == 15), grp="r", heads=(2, 3))
                STAGE[0] = "P2.ret.gT"
                for half in range(2):
                    bk, bR = bank("r")
                    bkb = bk[:].bitcast(BF16)
                    for j in range(8):
                        kc = half * 8 + j
                        hh = kc // 4
                        PE.op(lambda: nc.tensor.transpose(bkb[:, j * 128:(j + 1) * 128],
                                                          gated[:, 4 * gb_ + hh, (kc % 4) * 128:(kc % 4 + 1) * 128],
                                                          identb[:]),
                              reads=[gated_R[gb_][hh], R_const], writes=[bR], signal=(j == 7))
                    ACT.op(lambda: nc.scalar.copy(gT[:, half * 8:half * 8 + 8, cs_],
                                                  bkb.rearrange("p (j i) -> p j i", j=8)),
                           reads=[bR], writes=gT_R[half * 8:half * 8 + 8])
            if t + 1 < NT:
                p2_n1a(t + 1)
            STAGE[0] = "P2.wo"
            gT_k = [gT[:, j, :] for j in range(16)]
            for i in range(4):
                slab, sR = w_next(("wo", i))

                def evac(mm, bk, bR, i=i):
                    m = i * 2 + mm
                    DVE.op(lambda: nc.vector.tensor_tensor(XK[m], bk[:], XK[m], ALU.add),
                           reads=[bR, XR[m]], writes=[XR[m]])
                proj_fm(slab, sR, 16, 2, gT_k, gT_R, evac, warm=(i == 0))
            STAGE[0] = "P2.mlp"
            mlp("up1", "dn1", g_mlp1, XK, XR, mid_hook=(lambda: p2_n1b(t + 1)) if t + 1 < NT else None)
            if t + 1 < NT:
                p2_prefetch_k(t + 1)
            STAGE[0] = "P2.N3out"
            rmsnorm(T, XK, XR, g_fin, XK, XR, rstd[:, 0, :], rstd_R[0],
                    sq=[hm[:, k, :] for k in range(8)], sq_R=[HR[k] for k in range(8)])
            dq = []
            if t + 1 < NT:
                p2_q(t + 1, slabs=(0,), deferred=dq)
            STAGE[0] = "P2.out"
            for c in range(4):
                for half in range(2):
                    bk, bR = bank("side")
                    for j in range(4):
                        k = half * 4 + j
                        PE.op(lambda: nc.tensor.transpose(bk[:, j * 128:(j + 1) * 128],
                                                          XK[k][:, c * 128:(c + 1) * 128], ident[:]),
                              reads=[XR[k], R_id], writes=[bR], signal=(j == 3))
                    yb = c % 2
                    if half == 0:
                        ACT.op(lambda: nc.scalar.copy(youts[yb][:, 0:512], bk[:]), reads=[bR], writes=[youts_R[yb][0]])
                    else:
                        DVE.op(lambda: nc.vector.tensor_copy(youts[yb][:, 512:1024], bk[:]), reads=[bR],
                               writes=[youts_R[yb][1]])
                r0 = t * T + c * 128
                tok = SP.dma(c_yout[yb], y_out[r0:r0 + 128, :], youts[yb], reads=youts_R[yb])
                out_toks.append(tok)
            for run in dq:
                run()
            if t + 1 < NT:
                p2_q(t + 1, slabs=(1,))
                act_preload(AF.Silu)

        for tok in out_toks[-2:]:
            SP._wait(tok)
        assert wst["next"] == len(seq)
    return nc


_NC_CACHE = {}


def _small_pack(inp):
    def fm(v):
        return np.ascontiguousarray(np.asarray(v, np.float32).reshape(8, 128).T)
    cols = [fm(inp["norm_mix_0"]), fm(inp["norm_mlp_0"]), fm(inp["norm_mix_1"]), fm(inp["norm_mlp_1"]),
            fm(inp["norm_final"])]
    cwv = np.asarray(inp["conv_w_0"], np.float32)
    for j in range(3):
        cols.append(fm(cwv[j]))
    cols.append(fm(inp["conv_b_0"]))
    return np.ascontiguousarray(np.concatenate(cols, axis=1))


def kernel(**inputs):
    if "nc" not in _NC_CACHE:
        _NC_CACHE["nc"] = build_nc()
    nc = _NC_CACHE["nc"]
    xp = np.asarray(inputs["x_prompt"], np.float32)
    xs = np.asarray(inputs["x_sample"], np.float32)
    small = _small_pack(inputs)
    half = 128
    rope_inv = (np.float32(10000.0) ** (-np.arange(half, dtype=np.float32) / np.float32(half))).astype(np.float32)
    rope_inv = np.ascontiguousarray(rope_inv.reshape(128, 1))
    wnames = ["w_in_conv_0", "w_out_conv_0", "w_up_0", "w_down_0", "w_qkvg_1", "w_o_1", "w_up_1", "w_down_1"]
    shared = {n: np.ascontiguousarray(np.asarray(inputs[n], np.float32)) for n in wnames}
    in_maps = []
    for c in range(8):
        if c < 4:
            x = xp[c]
            flag = np.tile(np.array([[1.0, 0.0]], np.float32), (128, 1))
        else:
            x = xs[2 * (c - 4):2 * (c - 4) + 2].reshape(NTOK, D)
            flag = np.tile(np.array([[0.0, 2048.0]], np.float32), (128, 1))
        m = dict(shared)
        m["x"] = np.ascontiguousarray(x)
        m["small"] = small
        m["flag"] = np.ascontiguousarray(flag)
        m["rope_inv"] = rope_inv
        in_maps.append(m)
    res = run_bass_kernel_spmd(nc, in_maps, core_ids=list(range(8)))
    ys = [np.asarray(r["y"], np.float32) for r in res.results]
    y_prompt = np.stack(ys[0:4], axis=0)
    y_sample = np.stack([ys[4 + i].reshape(2, 2048, D) for i in range(4)], axis=0).reshape(8, 2048, D)
    if DEBUG:
        kernel.debug = res.results
    return (y_prompt, y_sample)
```
